# Optimizing a Trainium2 kernel written in Bass

```python
import jax, jax.numpy as jnp
from jax import lax
import numpy as np

D_MODEL = 2048
BATCH = 4
SEQ = 4096
DEPTH = 2

N_META = 16
N_MIXERS = 2
N_HEADS = 16
HEAD_DIM = D_MODEL // N_HEADS
ATTN_WIDTH = N_HEADS * HEAD_DIM
FOX_IN_COLS = 4 * ATTN_WIDTH + N_HEADS
Q_BLOCK = 128
POOL_WINDOWS = (2, 4, 8, 16)
N_POOL_GROUPS = len(POOL_WINDOWS)
POOL_WIDTH = D_MODEL
POOL_GROUP = POOL_WIDTH // N_POOL_GROUPS
POOL_IN_COLS = 2 * POOL_WIDTH
ALPHA = (2.0 * DEPTH) ** 0.25
BETA = (8.0 * DEPTH) ** -0.25
LN_EPS = 1e-5

kernel_name = "fox_pool_interleaved_deepnorm"


def layer_norm(x, g, b):
    xf = x.astype(jnp.float32)
    mu = jnp.mean(xf, axis=-1, keepdims=True)
    var = jnp.mean(jnp.square(xf - mu), axis=-1, keepdims=True)
    y = (xf - mu) * lax.rsqrt(var + LN_EPS)
    return (y * g.astype(jnp.float32) + b.astype(jnp.float32)).astype(x.dtype)


def _fox_attend(q_blk, c_q, q_pos, k, v, c_k, k_pos):
    s = jnp.einsum('bhqd,bhkd->bhqk', q_blk, k).astype(jnp.float32) * (HEAD_DIM ** -0.5)
    s = s + c_q[..., :, None] - c_k[..., None, :]
    mask = k_pos[None, :] <= q_pos[:, None]
    s = jnp.where(mask[None, None], s, -jnp.inf)
    p = jax.nn.softmax(s, axis=-1)
    return jnp.einsum('bhqk,bhkd->bhqd', p.astype(v.dtype), v)


def fox_mixer(h, w_in, b_f, w_out):
    B, L, _ = h.shape
    proj = h @ w_in
    q, k, v, z, f = jnp.split(proj, [ATTN_WIDTH, 2 * ATTN_WIDTH, 3 * ATTN_WIDTH, 4 * ATTN_WIDTH], axis=-1)
    to_heads = lambda t: t.reshape(B, L, N_HEADS, HEAD_DIM).transpose(0, 2, 1, 3)
    q, k, v = to_heads(q), to_heads(k), to_heads(v)
    log_f = jax.nn.log_sigmoid(f.astype(jnp.float32) + b_f.astype(jnp.float32))
    c = jnp.cumsum(log_f, axis=1).transpose(0, 2, 1)
    k_pos = jnp.arange(L)
    o_meta = _fox_attend(q[:, :, :N_META], c[:, :, :N_META], k_pos[:N_META],
                         k[:, :, :N_META], v[:, :, :N_META], c[:, :, :N_META], k_pos[:N_META])
    n_blk = (L - N_META) // Q_BLOCK
    q_r = q[:, :, N_META:].reshape(B, N_HEADS, n_blk, Q_BLOCK, HEAD_DIM).transpose(2, 0, 1, 3, 4)
    c_r = c[:, :, N_META:].reshape(B, N_HEADS, n_blk, Q_BLOCK).transpose(2, 0, 1, 3)

    def block(args):
        q_blk, c_blk, idx = args
        q_pos = N_META + idx * Q_BLOCK + jnp.arange(Q_BLOCK)
        return _fox_attend(q_blk, c_blk, q_pos, k, v, c, k_pos)

    o_r = lax.map(block, (q_r, c_r, jnp.arange(n_blk)))
    o_r = o_r.transpose(1, 2, 0, 3, 4).reshape(B, N_HEADS, n_blk * Q_BLOCK, HEAD_DIM)
    o = jnp.concatenate([o_meta, o_r], axis=2).transpose(0, 2, 1, 3).reshape(B, L, ATTN_WIDTH)
    return (o * jax.nn.silu(z)) @ w_out


def pool_mixer(h, w_in, w_grp, scale, w_out):
    B, L, _ = h.shape
    proj = h @ w_in
    u, z = jnp.split(proj, [POOL_WIDTH], axis=-1)
    uf = u.astype(jnp.float32).reshape(B, L, N_POOL_GROUPS, POOL_GROUP)
    t1 = jnp.arange(1, L + 1, dtype=jnp.float32)
    outs = []
    for g, w in enumerate(POOL_WINDOWS):
        ug = uf[:, :, g]
        cs = jnp.pad(jnp.cumsum(ug, axis=1), ((0, 0), (w, 0), (0, 0)))
        win_sum = cs[:, w:] - cs[:, :L]
        cnt = jnp.minimum(t1, float(w))[None, :, None]
        outs.append(win_sum / cnt - ug)
    d = jnp.stack(outs, axis=2)
    d = jnp.einsum('blgc,gce->blge', d, w_grp.astype(jnp.float32)).reshape(B, L, POOL_WIDTH)
    d = (d * scale.astype(jnp.float32)).astype(z.dtype)
    return (d * jax.nn.silu(z)) @ w_out


def setup_inputs(seed: int = 0) -> dict:
    key = jax.random.key(seed)
    ks = jax.random.split(key, 14)
    n = jax.random.normal
    f32 = jnp.float32
    return {
        "x": n(ks[0], (BATCH, SEQ, D_MODEL), f32),
        "meta_tokens": n(ks[1], (N_META, D_MODEL), f32),
        "fox_w_in": n(ks[2], (D_MODEL, FOX_IN_COLS), f32) * D_MODEL ** -0.5,
        "fox_b_f": 2.0 + 0.5 * n(ks[3], (N_HEADS,), f32),
        "fox_w_out": n(ks[4], (ATTN_WIDTH, D_MODEL), f32) * ATTN_WIDTH ** -0.5 * BETA,
        "ln0_g": 1.0 + 0.02 * n(ks[5], (D_MODEL,), f32),
        "ln0_b": 0.02 * n(ks[6], (D_MODEL,), f32),
        "pool_w_in": n(ks[7], (D_MODEL, POOL_IN_COLS), f32) * D_MODEL ** -0.5,
        "pool_w_grp": n(ks[8], (N_POOL_GROUPS, POOL_GROUP, POOL_GROUP), f32) * POOL_GROUP ** -0.5,
        "pool_scale": 1.0 + 0.1 * n(ks[9], (POOL_WIDTH,), f32),
        "pool_w_out": n(ks[10], (POOL_WIDTH, D_MODEL), f32) * POOL_WIDTH ** -0.5 * BETA,
        "ln1_g": 1.0 + 0.02 * n(ks[11], (D_MODEL,), f32),
        "ln1_b": 0.02 * n(ks[12], (D_MODEL,), f32),
    }


def reference(x, meta_tokens, fox_w_in, fox_b_f, fox_w_out, ln0_g, ln0_b,
              pool_w_in, pool_w_grp, pool_scale, pool_w_out, ln1_g, ln1_b):
    B = x.shape[0]
    meta = jnp.broadcast_to(meta_tokens[None].astype(x.dtype), (B, N_META, D_MODEL))
    h = jnp.concatenate([meta, x], axis=1)
    norms = ((ln0_g, ln0_b), (ln1_g, ln1_b))
    for i in range(DEPTH):
        if i % N_MIXERS == 0:
            y = fox_mixer(h, fox_w_in, fox_b_f, fox_w_out)
        else:
            y = pool_mixer(h, pool_w_in, pool_w_grp, pool_scale, pool_w_out)
        g, b = norms[i]
        h = layer_norm(ALPHA * h + y, g, b)
    return h[:, N_META:]
```

```python
import numpy as np
from contextlib import ExitStack
import concourse.bass as bass
import concourse.mybir as mybir
from concourse.bass_utils import run_bass_kernel_spmd

F32 = mybir.dt.float32
BF16 = mybir.dt.bfloat16
AF = mybir.ActivationFunctionType
ALU = mybir.AluOpType
AX = mybir.AxisListType

D = 2048
NDC = 16
L = 4112
NOWN = 2112
SB = 528
NH = 16
NKB = 33
ALPHA = 4.0 ** 0.25
LN_EPS = 1e-5
QSCALE = 128.0 ** -0.5
NEG = -30000.0
FOXC = 8208


def kb_range(kb):
    return (0, 16) if kb == 0 else (16 + 128 * (kb - 1), 128)


class Sem:
    __slots__ = ("h", "i", "h_pool")

    def __init__(self, h, i):
        self.h = h
        self.i = i
        self.h_pool = False


class Buf:
    __slots__ = ("name", "w", "r", "war", "dsem", "dcnt")

    def __init__(self, name):
        self.name = name
        self.w = {}
        self.r = {}
        self.war = {}
        self.dsem = None
        self.dcnt = 0


class Eng:
    def __init__(self, h, sem):
        self.h = h
        self.sem = sem
        self.cnt = 0
        self.seen = {}


class Sched:
    def __init__(self, nc, es):
        self.nc = nc
        self.es = es
        self.nsem = 0
        self.sems = {}
        self.free_dsems = []
        self.all_dbufs = []
        self.E = {}
        for k, h in (("pe", nc.tensor), ("act", nc.scalar), ("dve", nc.vector),
                     ("pool", nc.gpsimd), ("sp", nc.sync)):
            self.E[k] = Eng(h, self.newsem("e_" + k))

    def newsem(self, name):
        h = self.es.enter_context(self.nc.semaphore(name))
        s = Sem(h, self.nsem)
        self.sems[self.nsem] = s
        self.nsem += 1
        return s

    def _merge(self, toks, d):
        for i, v in d.items():
            if toks.get(i, 0) < v:
                toks[i] = v

    def _wait(self, e, toks):
        for i, v in toks.items():
            if e.seen.get(i, 0) >= v:
                continue
            e.h.wait_ge(self.sems[i].h, v)
            e.seen[i] = v

    def _deps(self, reads, writes, pwrites):
        toks = {}
        for b in reads:
            self._merge(toks, b.w)
        for b in writes:
            self._merge(toks, b.w)
            self._merge(toks, b.r)
            self._merge(toks, b.war)
        for b in pwrites:
            self._merge(toks, b.r)
            self._merge(toks, b.war)
        return toks

    def _commit(self, tok, reads, writes, pwrites):
        i, v = tok
        for b in reads:
            if b.r.get(i, 0) < v:
                b.r[i] = v
        for b in list(writes) + list(pwrites):
            if b.w.get(i, 0) < v:
                b.w[i] = v
            if b.r:
                self._merge(b.war, b.r)
                b.r = {}

    def op(self, eng, fn, reads=(), writes=(), pwrites=()):
        e = self.E[eng]
        toks = self._deps(reads, writes, pwrites)
        if eng == "pe":
            toks.pop(e.sem.i, None)
        self._wait(e, toks)
        ins = fn(e.h)
        e.cnt += 1
        ins.then_inc(e.sem.h, 1)
        self._commit((e.sem.i, e.cnt), reads, writes, pwrites)

    def dma(self, q, out, in_, sbuf, reads=(), writes=(), pwrites=()):
        e = self.E[q]
        toks = self._deps(reads, writes, pwrites)
        self._wait(e, toks)
        if sbuf.dsem is None:
            if self.free_dsems and q != "pool":
                sbuf.dsem, sbuf.dcnt = self.free_dsems.pop()
            else:
                sbuf.dsem = self.newsem("d_" + sbuf.name)
                sbuf.dcnt = 0
            self.all_dbufs.append(sbuf)
        if q == "pool":
            sbuf.dsem.h_pool = True
        e.h.dma_start(out=out, in_=in_).then_inc(sbuf.dsem.h, 16)
        sbuf.dcnt += 16
        self._commit((sbuf.dsem.i, sbuf.dcnt), reads, writes, pwrites)

    def barrier(self, release=()):
        toks = {}
        for e in self.E.values():
            if e.cnt:
                toks[e.sem.i] = e.cnt
        for b in self.all_dbufs:
            if b.dsem is not None:
                toks[b.dsem.i] = max(toks.get(b.dsem.i, 0), b.dcnt)
        for e in self.E.values():
            self._wait(e, toks)
        for b in release:
            if b.dsem is not None:
                if not b.dsem.h_pool:
                    self.free_dsems.append((b.dsem, b.dcnt))
                self.all_dbufs.remove(b)
                b.dsem = None


def build(upto=99, dbg=False):
    nc = bass.Bass("TRN2", target_bir_lowering=False)

    def din(name, shape, dt=F32):
        return nc.dram_tensor(name, list(shape), dt, kind="ExternalInput").ap()

    def dscr(name, shape, dt, out=False):
        kind = "ExternalOutput" if (out and dbg) else "Internal"
        return nc.dram_tensor(name, list(shape), dt, kind=kind).ap()

    xallT = din("xallT", [128, NDC, L])
    xownT = din("xownT", [128, NDC, NOWN])
    xown = din("xown", [NOWN, D])
    w_in = din("fox_w_in", [D, FOXC])
    b_f = din("fox_b_f", [NH, 1])
    w_o0 = din("fox_w_out", [D, D])
    ln0g = din("ln0_g", [D])
    ln0b = din("ln0_b", [D])
    pw_in = din("pool_w_in", [D, 2 * D])
    pw_grp = din("pool_w_grp", [4, 512, 512])
    p_scale = din("pool_scale", [128, NDC])
    pw_out = din("pool_w_out", [D, D])
    ln1g = din("ln1_g", [D])
    ln1b = din("ln1_b", [D])
    ident_in = din("ident", [128, 128])
    ind_in = din("ind", [4, L])
    rowm_in = din("rowmask", [1, 10 * SB])
    tri_in = din("tri", [128, 9 * 128])
    ht_in = din("ht", [128, 10 * 16])
    out = nc.dram_tensor("out", [2048, D], F32, kind="ExternalOutput").ap()

    Kt_s = dscr("Kt_s", [NH, 128, L], BF16, out=True)
    V_s = dscr("V_s", [NH, 128, NKB * 128], BF16, out=True)
    Q_s = dscr("Q_s", [NH, 128, NOWN], BF16, out=True)
    Z_s = dscr("Z_s", [NH, 128, NOWN], BF16, out=True)
    C_s = dscr("C_s", [NH, 3 * NOWN], BF16, out=True)
    CK_s = dscr("CK_s", [128, NKB * NH], F32, out=True)
    OG_s = dscr("OG_s", [128, NH * NOWN], BF16, out=True)
    H1_s = dscr("H1_s", [NOWN, D], F32, out=True)
    H1T_s = dscr("H1T_s", [128, NDC, NOWN], BF16, out=True)

    w_in_v = w_in.rearrange("(dc p) c -> p dc c", p=128)
    pw_in_v = pw_in.rearrange("(dc p) c -> p dc c", p=128)

    with ExitStack() as es:
        S = Sched(nc, es)

        def sb(stack, name, shape, dt):
            return stack.enter_context(nc.sbuf_tensor("s_" + name, list(shape), dt))

        def ps(stack, name, shape, dt=F32):
            return stack.enter_context(nc.psum_tensor("p_" + name, list(shape), dt))

        B_xall, B_xown, B_w = Buf("xall"), Buf("xown"), Buf("w")
        B_Kt, B_V, B_Q, B_Z, B_C, B_CK = (Buf(n) for n in ("Kt", "V", "Q", "Z", "C", "CK"))
        B_OG, B_H1, B_H1T, B_out = Buf("OG"), Buf("H1"), Buf("H1T"), Buf("out")

        ident = sb(es, "ident", [128, 128], F32)
        identb = sb(es, "identb", [128, 128], BF16)
        ones4 = sb(es, "ones4", [4, 128], BF16)
        onesf = sb(es, "onesf", [128, 128], F32)
        ckT = sb(es, "ckT", [128, NKB * NH], F32)
        base = sb(es, "base", [16, 4], F32)
        nbf = sb(es, "nbf", [16, 1], F32)
        T_ident, T_identb, T_ones4, T_onesf, T_ckT, T_base, T_nbf = (
            Buf(n) for n in ("ident", "identb", "ones4", "onesf", "ckT", "base", "nbf"))
        S.dma("sp", ident[:], ident_in, T_ident, reads=[B_w], writes=[T_ident])
        S.dma("pool", identb[:], ident_in, T_identb, reads=[B_w], writes=[T_identb])
        S.dma("sp", nbf[:], b_f, T_nbf, reads=[B_w], writes=[T_nbf])
        S.op("dve", lambda v: v.memset(ones4[:], 1.0), writes=[T_ones4])
        S.op("dve", lambda v: v.memset(onesf[:], 1.0), writes=[T_onesf])
        S.op("dve", lambda v: v.tensor_scalar(nbf[:], nbf[:], -1.0, None, ALU.mult),
             reads=[T_nbf], writes=[T_nbf])

        evac_rr = [0]

        def evac(out_ap, in_ap, reads, writes=(), pwrites=(), scale=None, eng=None):
            if eng is None:
                eng = "act" if (evac_rr[0] & 1) else "dve"
                evac_rr[0] += 1
            if eng == "act":
                if scale is None:
                    S.op("act", lambda a: a.copy(out_ap, in_ap), reads, writes, pwrites)
                else:
                    S.op("act", lambda a: a.mul(out_ap, in_ap, scale), reads, writes, pwrites)
            else:
                if scale is None:
                    S.op("dve", lambda v: v.tensor_copy(out_ap, in_ap), reads, writes, pwrites)
                else:
                    S.op("dve", lambda v: v.tensor_scalar(out_ap, in_ap, scale, None, ALU.mult),
                         reads, writes, pwrites)

        def load_transposed(stack, src, B_src, ntok, xT, T_xT, tag):
            with ExitStack() as ph:
                NXR = 4
                xrow = [sb(ph, f"xrow{tag}{i}", [128, D], F32) for i in range(NXR)]
                T_xrow = [Buf(f"xrow{tag}{i}") for i in range(NXR)]
                pst = [ps(ph, f"pst{tag}{i}", [128, 512]) for i in range(2)]
                T_pst = [Buf(f"pst{tag}{i}") for i in range(2)]
                nblk = (ntok + 127) // 128
                for blk in range(nblk):
                    r0 = blk * 128
                    n = min(128, ntok - r0)
                    xr, Tx = xrow[blk % NXR], T_xrow[blk % NXR]
                    S.dma("sp", xr[0:n, :], src[r0:r0 + n, :], Tx, reads=[B_src], writes=[Tx])
                    for g4 in range(4):
                        pt, Tp = pst[g4 % 2], T_pst[g4 % 2]

                        def tr(pe, pt=pt, g4=g4, xr=xr, n=n):
                            ins = None
                            for q in range(4):
                                dc = 4 * g4 + q
                                ins = pe.transpose(pt[:, q * 128:q * 128 + n],
                                                   xr[0:n, dc * 128:(dc + 1) * 128], ident[0:n, 0:n])
                            return ins
                        S.op("pe", tr, reads=[Tx, T_ident], writes=[Tp])
                        src_ap = pt[:, :].rearrange("p (a b) -> p a b", a=4)[:, :, 0:n]
                        evac(xT[:, 4 * g4:4 * g4 + 4, r0:r0 + n], src_ap, reads=[Tp], pwrites=[T_xT])
                S.barrier(release=T_xrow)

        def logf_from(stack, xT, T_xT, ntok, tiles, lf, T_lf, wf, T_wf, psf, T_psf):
            for (p0, n) in tiles:
                def mm(pe, p0=p0, n=n):
                    ins = None
                    for dc in range(NDC):
                        ins = pe.matmul(psf[0:16, 0:n], wf[:, dc, :], xT[:, dc, p0:p0 + n],
                                        start=(dc == 0), stop=(dc == NDC - 1))
                    return ins
                S.op("pe", mm, reads=list(T_xT) + [T_wf], writes=[T_psf])
                S.op("act", lambda a, p0=p0, n=n: a.activation(
                    lf[:, p0:p0 + n], psf[0:16, 0:n], AF.Exp, bias=nbf[:, 0:1], scale=-1.0),
                    reads=[T_psf, T_nbf], pwrites=[T_lf])
            S.op("act", lambda a: a.activation(lf[:, 0:ntok], lf[:, 0:ntok], AF.Ln, bias=1.0, scale=1.0),
                 reads=[T_lf], writes=[T_lf])
            S.op("dve", lambda v: v.tensor_scalar(lf[:, 0:ntok], lf[:, 0:ntok], -1.0, None, ALU.mult),
                 reads=[T_lf], writes=[T_lf])

        with ExitStack() as phA:
            xT = sb(phA, "xT", [128, NDC, L], BF16)
            T_xTt = [Buf(f"xT{t}") for t in range(9)]
            T_xT = T_xTt

            def xa_bufs(p0, n):
                return [T_xTt[t] for t in range(p0 // 512, (p0 + n - 1) // 512 + 1)]

            def load_x_tiles():
                for t in range(9):
                    p0, n = (512 * t, 512) if t < 8 else (4096, 16)
                    S.dma("pool", xT[:, :, p0:p0 + n], xallT[:, :, p0:p0 + n], T_xTt[t],
                          reads=[B_xall], writes=[T_xTt[t]])

            with ExitStack() as ph:
                Wc = [sb(ph, f"WcA{i}", [128, NDC, 512], BF16) for i in range(2)]
                T_Wc = [Buf(f"WcA{i}") for i in range(2)]
                KTs = [sb(ph, f"KTs{i}", [128, L], BF16) for i in range(1)]
                T_KTs = [Buf(f"KTs{i}") for i in range(1)]
                VS = sb(ph, "VS", [128, 4, NKB, 128], BF16)
                T_VS = Buf("VS")
                T_VSz = Buf("VSz")
                pk = [ps(ph, f"pkA{i}", [128, 512]) for i in range(4)]
                T_pk = [Buf(f"pkA{i}") for i in range(4)]
                VSf = VS[:].rearrange("p a b c -> p (a b c)")
                chunks = [("k", hg) for hg in range(4)] + [("v", hg) for hg in range(4)]
                if upto < 2:
                    chunks = []

                def load_chunk(ci):
                    kind, hg = chunks[ci]
                    c0 = (2048 if kind == "k" else 4096) + 512 * hg
                    S.dma("pool", Wc[ci % 2][:], w_in_v[:, :, c0:c0 + 512], T_Wc[ci % 2],
                          reads=[B_w], writes=[T_Wc[ci % 2]])
                if chunks:
                    load_chunk(0)
                load_x_tiles()
                pki = 0
                nks = 0
                for ci, (kind, hg) in enumerate(chunks):
                    if ci + 1 < len(chunks):
                        load_chunk(ci + 1)
                    W, TW = Wc[ci % 2], T_Wc[ci % 2]
                    if kind == "v" and hg == 0:
                        S.op("pool", lambda g: g.memset(VSf, 0.0), writes=[T_VS, T_VSz])
                    if kind == "k" and hg == 0:
                        for t in range(9):
                            p0, n = (512 * t, 512) if t < 8 else (4096, 16)
                            for hh in range(4):
                                pt, Tp = pk[pki % 4], T_pk[pki % 4]
                                pki += 1

                                def mm(pe, pt=pt, W=W, hh=hh, p0=p0, n=n):
                                    ins = None
                                    for dc in range(NDC):
                                        ins = pe.matmul(pt[:, 0:n], W[:, dc, 128 * hh:128 * hh + 128],
                                                        xT[:, dc, p0:p0 + n], start=(dc == 0), stop=(dc == NDC - 1))
                                    return ins
                                S.op("pe", mm, reads=xa_bufs(p0, n) + [TW], writes=[Tp])
                                evac(VSf[:, hh * L + p0:hh * L + p0 + n], pt[:, 0:n], reads=[Tp], pwrites=[T_VS])
                        for hh in range(4):
                            S.dma("sp", Kt_s[hh], VSf[:, hh * L:(hh + 1) * L], T_VS, reads=[T_VS], pwrites=[B_Kt])
                    elif kind == "k":
                        for hh in range(4):
                            h = 4 * hg + hh
                            kt, Tk = KTs[0], T_KTs[0]
                            nks += 1
                            for t in range(9):
                                p0, n = (512 * t, 512) if t < 8 else (4096, 16)
                                pt, Tp = pk[pki % 4], T_pk[pki % 4]
                                pki += 1

                                def mm(pe, pt=pt, W=W, hh=hh, p0=p0, n=n):
                                    ins = None
                                    for dc in range(NDC):
                                        ins = pe.matmul(pt[:, 0:n], W[:, dc, 128 * hh:128 * hh + 128],
                                                        xT[:, dc, p0:p0 + n], start=(dc == 0), stop=(dc == NDC - 1))
                                    return ins
                                S.op("pe", mm, reads=xa_bufs(p0, n) + [TW], writes=[Tp])
                                evac(kt[:, p0:p0 + n], pt[:, 0:n], reads=[Tp], pwrites=[Tk])
                            S.dma("sp", Kt_s[h], kt[:], Tk, reads=[Tk], pwrites=[B_Kt])
                    else:
                        for kb in range(NKB):
                            p0, n = kb_range(kb)
                            pt, Tp = pk[pki % 4], T_pk[pki % 4]
                            pki += 1

                            def mm(pe, pt=pt, W=W, p0=p0, n=n):
                                ins = None
                                for dc in range(NDC):
                                    ins = pe.matmul(pt[0:n, :], xT[:, dc, p0:p0 + n], W[:, dc, :],
                                                    start=(dc == 0), stop=(dc == NDC - 1))
                                return ins
                            S.op("pe", mm, reads=xa_bufs(p0, n) + [TW], writes=[Tp])
                            evac(VS[0:n, :, kb, :], pt[0:n, :].rearrange("p (a b) -> p a b", a=4),
                                 reads=[Tp, T_VSz], pwrites=[T_VS])
                        for hh in range(4):
                            S.dma("sp", V_s[4 * hg + hh], VS[:, hh].rearrange("p a b -> p (a b)"), T_VS,
                                  reads=[T_VS], pwrites=[B_V])
                S.barrier(release=T_Wc + T_KTs + [T_VS])
            with ExitStack() as ph:
                wf = sb(ph, "wf", [128, NDC, 16], BF16)
                lf = sb(ph, "lf", [16, L], F32)
                cc = sb(ph, "cc", [16, L], F32)
                tmp = sb(ph, "ctmp", [16, L], F32)
                ind = sb(ph, "ind", [16, L], F32)
                psf = ps(ph, "psf", [128, 512])
                psc = ps(ph, "psc", [128, 1024])
                T_wf, T_lf, T_cc, T_tmp, T_ind, T_psf, T_psc = (
                    Buf(n) for n in ("wf", "lf", "cc", "tmp", "ind", "psf", "psc"))
                S.dma("pool", wf[:], w_in_v[:, :, 8192:8208], T_wf, reads=[B_w], writes=[T_wf])
                tiles = [(512 * t, 512) for t in range(8)] + [(4096, 16)]
                logf_from(ph, xT, T_xT, L, tiles, lf, T_lf, wf, T_wf, psf, T_psf)
                S.op("dve", lambda v: v.memset(tmp[:], 1.0), writes=[T_tmp])
                S.op("dve", lambda v: v.tensor_tensor_scan(cc[:], tmp[:], lf[:], 0.0, ALU.mult, ALU.add),
                     reads=[T_tmp, T_lf], writes=[T_cc])
                for j in range(4):
                    S.dma("sp", ind[:], ind_in[j].partition_broadcast(16), T_ind, reads=[B_w], writes=[T_ind])
                    S.op("dve", lambda v, j=j: v.tensor_tensor(tmp[:], lf[:], ind[:], ALU.mult),
                         reads=[T_lf, T_ind], writes=[T_tmp])
                    S.op("dve", lambda v, j=j: v.reduce_sum(base[:, j:j + 1], tmp[:], AX.X),
                         reads=[T_tmp], pwrites=[T_base])
                def trc(pe):
                    ins = None
                    for kb in range(NKB):
                        p0, n = kb_range(kb)
                        ins = pe.transpose(psc[0:n, kb * 16:(kb + 1) * 16], cc[0:16, p0:p0 + n], ident[0:16, 0:16])
                    return ins
                S.op("pe", trc, reads=[T_cc, T_ident], writes=[T_psc])
                S.op("dve", lambda v: v.memset(ckT[:], 0.0), writes=[T_ckT])
                S.op("dve", lambda v: v.tensor_scalar(ckT[:, 16:NKB * NH], psc[:, 16:NKB * NH], -1.0, None, ALU.mult),
                     reads=[T_psc], pwrites=[T_ckT])
                S.op("dve", lambda v: v.tensor_scalar(ckT[0:16, 0:16], psc[0:16, 0:16], -1.0, None, ALU.mult),
                     reads=[T_psc], pwrites=[T_ckT])
                if dbg:
                    S.dma("sp", CK_s, ckT[:], T_ckT, reads=[T_ckT], pwrites=[B_CK])
                S.barrier(release=[T_wf, T_ind])

            S.barrier()

        if upto >= 3:
            with ExitStack() as phB:
                xTo = sb(phB, "xTo", [128, NDC, NOWN], BF16)
                T_xTo = [Buf(f"xTo{j}") for j in range(4)]

                def load_xo_tiles():
                    for j in range(4):
                        S.dma("pool", xTo[:, :, SB * j:SB * (j + 1)], xownT[:, :, SB * j:SB * (j + 1)], T_xTo[j],
                              reads=[B_xown], writes=[T_xTo[j]])
                with ExitStack() as ph:
                    Wc = [sb(ph, f"WcB{i}", [128, NDC, 512], BF16) for i in range(2)]
                    T_Wc = [Buf(f"WcB{i}") for i in range(2)]
                    QTs = [sb(ph, f"QTs{i}", [128, NOWN], BF16) for i in range(4)]
                    T_QTs = [Buf(f"QTs{i}") for i in range(4)]
                    pq = [ps(ph, f"pqB{i}", [128, 1024]) for i in range(3)]
                    T_pq = [Buf(f"pqB{i}") for i in range(3)]
                    chunks = [("q", hg) for hg in range(4)] + [("z", hg) for hg in range(4)]

                    def load_chunk(ci):
                        kind, hg = chunks[ci]
                        c0 = (0 if kind == "q" else 6144) + 512 * hg
                        S.dma("pool", Wc[ci % 2][:], w_in_v[:, :, c0:c0 + 512], T_Wc[ci % 2],
                              reads=[B_w], writes=[T_Wc[ci % 2]])
                    load_chunk(0)
                    load_xo_tiles()
                    pqi = 0
                    nqs = 0
                    for ci, (kind, hg) in enumerate(chunks):
                        if ci + 1 < len(chunks):
                            load_chunk(ci + 1)
                        W, TW = Wc[ci % 2], T_Wc[ci % 2]
                        for j in range(4):
                            for hh in range(4):
                                h = 4 * hg + hh
                                qt, Tq = QTs[hh], T_QTs[hh]
                                pt, Tp = pq[pqi % 3], T_pq[pqi % 3]
                                pqi += 1

                                def mm(pe, pt=pt, W=W, hh=hh, j=j):
                                    ins = None
                                    for (c0, o0, n) in ((0, SB * j, 512), (512, SB * j + 512, 16)):
                                        for dc in range(NDC):
                                            ins = pe.matmul(pt[:, c0:c0 + n], W[:, dc, 128 * hh:128 * hh + 128],
                                                            xTo[:, dc, o0:o0 + n], start=(dc == 0), stop=(dc == NDC - 1))
                                    return ins
                                S.op("pe", mm, reads=[T_xTo[j], TW], writes=[Tp])
                                if kind == "q":
                                    evac(qt[:, SB * j:SB * (j + 1)], pt[:, 0:SB], reads=[Tp], pwrites=[Tq],
                                         scale=QSCALE, eng="dve")
                                else:
                                    S.op("act", lambda a, qt=qt, pt=pt, j=j: a.activation(
                                        qt[:, SB * j:SB * (j + 1)], pt[:, 0:SB], AF.Silu),
                                        reads=[Tp], pwrites=[Tq])
                        for hh in range(4):
                            dst = Q_s if kind == "q" else Z_s
                            S.dma("sp", dst[4 * hg + hh], QTs[hh][:], T_QTs[hh], reads=[T_QTs[hh]],
                                  pwrites=[B_Q if kind == "q" else B_Z])
                        if ci == 0:
                            wf = sb(ph, "wfb", [128, NDC, 16], BF16)
                            lf = sb(ph, "lfb", [16, NOWN], F32)
                            cc = sb(ph, "ccb", [16, NOWN], F32)
                            t1 = sb(ph, "t1b", [16, NOWN], F32)
                            c3 = sb(ph, "c3b", [16, 3, NOWN], BF16)
                            psf = ps(ph, "psfb", [128, 512])
                            T_wf, T_lf, T_cc, T_t1, T_c3, T_psf = (Buf(n) for n in ("wfb", "lfb", "ccb", "t1b", "c3b", "psfb"))
                            S.dma("pool", wf[:], w_in_v[:, :, 8192:8208], T_wf, reads=[B_w], writes=[T_wf])
                            tiles = []
                            for j in range(4):
                                tiles += [(SB * j, 512), (SB * j + 512, 16)]
                            logf_from(ph, xTo, T_xTo, NOWN, tiles, lf, T_lf, wf, T_wf, psf, T_psf)
                            S.op("dve", lambda v: v.memset(t1[:], 1.0), writes=[T_t1])
                            for j in range(4):
                                S.op("dve", lambda v, j=j: v.tensor_tensor_scan(
                                    cc[:, SB * j:SB * (j + 1)], t1[:, SB * j:SB * (j + 1)], lf[:, SB * j:SB * (j + 1)],
                                    base[:, j:j + 1], ALU.mult, ALU.add),
                                    reads=[T_t1, T_lf, T_base], pwrites=[T_cc])
                            S.op("dve", lambda v: v.tensor_copy(c3[:, 0, :], cc[:]), reads=[T_cc], pwrites=[T_c3])
                            S.op("dve", lambda v: v.tensor_copy(t1[:], c3[:, 0, :]), reads=[T_c3], writes=[T_t1])
                            S.op("dve", lambda v: v.tensor_tensor(cc[:], cc[:], t1[:], ALU.subtract), reads=[T_t1, T_cc], writes=[T_cc])
                            S.op("dve", lambda v: v.tensor_copy(c3[:, 1, :], cc[:]), reads=[T_cc], pwrites=[T_c3])
                            S.op("dve", lambda v: v.tensor_copy(t1[:], c3[:, 1, :]), reads=[T_c3], writes=[T_t1])
                            S.op("dve", lambda v: v.tensor_tensor(cc[:], cc[:], t1[:], ALU.subtract), reads=[T_t1, T_cc], writes=[T_cc])
                            S.op("dve", lambda v: v.tensor_copy(c3[:, 2, :], cc[:]), reads=[T_cc], pwrites=[T_c3])
                            S.dma("sp", C_s, c3[:].rearrange("p a b -> p (a b)"), T_c3, reads=[T_c3], pwrites=[B_C])
                    S.barrier(release=T_Wc + T_QTs + [T_wf, T_c3])
                S.barrier()

        if upto >= 4:
            with ExitStack() as phC:
                ogT = sb(phC, "ogT", [128, NH, NOWN], BF16)
                T_ogT = Buf("ogT")
                WoA = sb(phC, "WoA", [128, NDC, 1024], BF16)
                T_WoA = Buf("WoA")
                w_o0_v = w_o0.rearrange("(dc p) c -> p dc c", p=128)
                with ExitStack() as ph:
                    KT = [sb(ph, f"KT{i}", [128, L], BF16) for i in range(2)]
                    VV = [sb(ph, f"VV{i}", [128, NKB, 128], BF16) for i in range(2)]
                    QT = [sb(ph, f"QT{i}", [128, NOWN], BF16) for i in range(2)]
                    ZT = [sb(ph, f"ZT{i}", [128, NOWN], BF16) for i in range(2)]
                    RR = [sb(ph, f"RR{i}", [128, 10, SB], BF16) for i in range(2)]
                    onesb = sb(ph, "onesb", [128, 128], BF16)
                    T_onesb = Buf("onesb")
                    S.op("pool", lambda g: g.memset(onesb[:], 1.0), writes=[T_onesb])
                    TRI = sb(ph, "TRI", [128, 9, 128], BF16)
                    HT = sb(ph, "HT", [128, 10, 16], BF16)
                    PT = [sb(ph, f"PT{i}", [128, SB], BF16) for i in range(3)]
                    Pacc = [sb(ph, f"Pacc{i}", [128, SB], F32) for i in range(2)]
                    rL = sb(ph, "rL", [128, SB], F32)
                    T_KT = [Buf(f"KT{i}") for i in range(2)]
                    T_VV = [Buf(f"VV{i}") for i in range(2)]
                    T_QT = [Buf(f"QT{i}") for i in range(2)]
                    T_ZT = [Buf(f"ZT{i}") for i in range(2)]
                    T_RR = [Buf(f"RR{i}") for i in range(2)]
                    T_RRm = [Buf(f"RRm{i}") for i in range(2)]
                    T_TRI, T_HT, T_rL = Buf("TRI"), Buf("HT"), Buf("rL")
                    T_Pacc = [Buf(f"Pacc{i}") for i in range(2)]
                    T_PT = [Buf(f"PT{i}") for i in range(3)]
                    Sps = [ps(ph, f"Sps{i}", [128, 1024]) for i in range(2)]
                    Ops = [ps(ph, f"Ops{i}", [128, 1024]) for i in range(2)]
                    T_S = [Buf(f"Sps{i}") for i in range(2)]
                    T_O = [Buf(f"Ops{i}") for i in range(2)]

                    S.dma("pool", TRI[:].rearrange("p a b -> p (a b)"), tri_in, T_TRI, reads=[B_w], writes=[T_TRI])
                    S.dma("pool", HT[:].rearrange("p a b -> p (a b)"), ht_in, T_HT, reads=[B_w], writes=[T_HT])
                    for i in range(2):
                        S.op("pool", lambda g, i=i: g.memset(RR[i][:].rearrange("p a b -> p (a b)"), 0.0), writes=[T_RR[i]])
                        S.dma("pool", RR[i][3:4, :, :].rearrange("p a b -> p (a b)"), rowm_in, T_RRm[i],
                              reads=[B_w], writes=[T_RR[i]])
                    heads = list(range(NH)) if upto >= 5 else [0]
                    if upto >= 6:
                        for q in range(2):
                            S.dma("pool", WoA[:, :, 512 * q:512 * (q + 1)], w_o0_v[:, :, 512 * q:512 * (q + 1)], T_WoA,
                                  reads=[B_w], pwrites=[T_WoA])

                    def load_head(hi):
                        h = heads[hi]
                        i = hi % 2
                        S.dma("sp", KT[i][:], Kt_s[h], T_KT[i], reads=[B_Kt], writes=[T_KT[i]])
                        S.dma("sp", VV[i][:].rearrange("p a b -> p (a b)"), V_s[h], T_VV[i], reads=[B_V], writes=[T_VV[i]])
                        S.dma("sp", QT[i][:], Q_s[h], T_QT[i], reads=[B_Q], writes=[T_QT[i]])
                        S.dma("sp", ZT[i][:], Z_s[h], T_ZT[i], reads=[B_Z], writes=[T_ZT[i]])

                    units = [(hi, j) for hi in range(len(heads)) for j in range(4)]
                    items = []
                    for u, (hi, j) in enumerate(units):
                        nkb = 8 * j + 9
                        order = list(range(1, nkb)) + [0]
                        for idx, kb in enumerate(order):
                            items.append(("p", u, kb, idx, len(order)))
                            if idx == 1 and u > 0:
                                items.append(("L", u - 1, 0, 0, 0))
                    items.append(("L", len(units) - 1, 0, 0, 0))
                    unit_rr = {}

                    def unit_setup(u):
                        hi, j = units[u]
                        if j == 0 and hi == 0:
                            load_head(0)
                        if j == 1 and hi + 1 < len(heads):
                            load_head(hi + 1)
                        h = heads[hi]
                        rr, Trr = RR[u % 2], T_RR[u % 2]
                        csrc = C_s[h:h + 1, :].rearrange("o (a b) -> (o a) b", a=3)[:, SB * j:SB * (j + 1)]
                        S.dma("sp", rr[0:3, :, :], csrc.unsqueeze(1).broadcast_to([3, 10, SB]), Trr,
                              reads=[B_C], pwrites=[Trr])
                        unit_rr[u] = (rr, Trr)

                    def produce(g):
                        kind, u, kb, idx, nord = items[g]
                        hi, j = units[u]
                        i = hi % 2
                        sp_, Ts = Sps[g % 2], T_S[g % 2]
                        if kind == "L":
                            pacc, Tpa = Pacc[u % 2], T_Pacc[u % 2]

                            def lm(pe):
                                ins = None
                                for (c0, nn) in ((0, 512), (512, 16)):
                                    ins = pe.matmul(sp_[:, c0:c0 + nn], onesf[:, :], pacc[:, c0:c0 + nn], start=True, stop=True)
                                return ins
                            S.op("pe", lm, reads=[Tpa, T_onesf], writes=[Ts])
                            return
                        if idx == 0:
                            unit_setup(u)
                        rr, Trr = unit_rr[u]
                        kt, qt = KT[i], QT[i]
                        q0 = SB * j
                        p0, n = kb_range(kb)
                        r = kb - 8 * j
                        v = r + 1 if (kb >= 1 and r >= 0) else 0

                        def mm(pe):
                            ins = None
                            for (c0, nn) in ((0, 512), (512, 16)):
                                pe.matmul(sp_[0:n, c0:c0 + nn], kt[:, p0:p0 + n], qt[:, q0 + c0:q0 + c0 + nn],
                                          start=True, stop=False)
                            masks = []
                            if kb == 0 and j == 0:
                                masks.append((0, 16, identb[0:16, 0:16], HT[0:16, 9, 0:16]))
                            if kb >= 1 and r >= 0:
                                masks.append((0, 16, identb[:, :], HT[:, r, 0:16]))
                                if r >= 1:
                                    tc = 16 + 128 * ((r - 1) % 4)
                                    if tc + 128 <= 512:
                                        masks.append((tc, 128, identb[:, :], TRI[:, r, 0:128]))
                                    else:
                                        masks.append((tc, 512 - tc, identb[:, :], TRI[:, r, 0:512 - tc]))
                                        masks.append((512, tc + 128 - 512, identb[:, :], TRI[:, r, 512 - tc:128]))
                            for (c0, nn) in ((0, 512), (512, 16)):
                                last = not any(((mc0 < 512) == (c0 < 512)) for (mc0, _, _, _) in masks)
                                ins = pe.matmul(sp_[0:n, c0:c0 + nn], onesb[:, 0:n], rr[:, v, c0:c0 + nn],
                                                start=False, stop=last)
                            for mi, (mc0, mn, lh, rh) in enumerate(masks):
                                later_same_bank = any(((m2[0] < 512) == (mc0 < 512)) for m2 in masks[mi + 1:])
                                ins = pe.matmul(sp_[0:n, mc0:mc0 + mn], lh, rh, start=False, stop=not later_same_bank,
                                                skip_group_check=True)
                            return ins
                        S.op("pe", mm, reads=[T_KT[i], T_QT[i], Trr, T_onesb, T_identb, T_TRI, T_HT], writes=[Ts])

                    npair = [0]
                    pt_of = {}

                    def consume(g, part):
                        kind, u, kb, idx, nord = items[g]
                        hi, j = units[u]
                        i = hi % 2
                        h = heads[hi]
                        q0 = SB * j
                        sp_, Ts = Sps[g % 2], T_S[g % 2]
                        ops_, To = Ops[u % 2], T_O[u % 2]
                        pacc, Tpa = Pacc[u % 2], T_Pacc[u % 2]
                        if kind == "L":
                            zt = ZT[i]
                            if part == 0:
                                S.op("act", lambda a: a.activation(rL[:], sp_[:, 0:SB], AF.Ln), reads=[Ts], writes=[T_rL])
                                S.op("act", lambda a: a.activation(rL[:], rL[:], AF.Exp, scale=-1.0), reads=[T_rL], writes=[T_rL])
                                return
                            S.op("dve", lambda v: v.tensor_tensor(rL[:], rL[:], zt[:, q0:q0 + SB], ALU.mult),
                                 reads=[T_rL, T_ZT[i]], writes=[T_rL])
                            S.op("dve", lambda v: v.tensor_tensor(ogT[:, h, q0:q0 + SB], ops_[:, 0:SB], rL[:], ALU.mult),
                                 reads=[To, T_rL], pwrites=[T_ogT])
                            return
                        p0, n = kb_range(kb)
                        vv = VV[i]
                        if part == 0:
                            pt_of[g] = (PT[npair[0] % 3], T_PT[npair[0] % 3])
                            npair[0] += 1
                        pt_, Tp = pt_of[g]
                        if part == 0:
                            S.op("act", lambda a: a.activation(
                                pt_[0:n, :], sp_[0:n, 0:SB], AF.Exp, bias=ckT[0:n, kb * 16 + h:kb * 16 + h + 1], scale=1.0),
                                reads=[Ts, T_ckT], writes=[Tp])
                            return

                        def pv(pe):
                            ins = None
                            for (c0, nn) in ((0, 512), (512, 16)):
                                ins = pe.matmul(ops_[:, c0:c0 + nn], vv[0:n, kb, :], pt_[0:n, c0:c0 + nn],
                                                start=(idx == 0), stop=(idx == nord - 1))
                            return ins
                        S.op("pe", pv, reads=[Tp, T_VV[i]], writes=[To])
                        if idx == 0:
                            S.op("dve", lambda v: v.tensor_copy(pacc[:, :], pt_[:, :]), reads=[Tp], writes=[Tpa])
                        else:
                            S.op("dve", lambda v: v.tensor_tensor(pacc[0:n, :], pacc[0:n, :], pt_[0:n, :], ALU.add),
                                 reads=[Tp, Tpa], writes=[Tpa])

                    nit = len(items)
                    produce(0)
                    produce(1)
                    for g in range(nit):
                        if g == nit - 1:
                            produce(g)
                        consume(g, 0)
                        if g + 2 < nit - 1:
                            produce(g + 2)
                        consume(g, 1)
                    if dbg:
                        S.dma("sp", OG_s, ogT[:].rearrange("p a b -> p (a b)"), T_ogT, reads=[T_ogT], pwrites=[B_OG])
                    S.barrier(release=T_KT + T_VV + T_QT + T_ZT + T_RR + [T_TRI, T_HT])

                if upto >= 6:
                    with ExitStack() as ph:
                        WoB = sb(ph, "WoB", [128, NDC, 1024], BF16)
                        Gt = sb(ph, "Gt", [128, D], F32)
                        Bt = sb(ph, "Bt", [128, D], F32)
                        xb = [sb(ph, f"xb{i}", [128, D], F32) for i in range(2)]
                        tt = [sb(ph, f"tt_{i}", [128, D], F32) for i in range(2)]
                        hb = [sb(ph, f"hb{i}", [128, D], F32) for i in range(2)]
                        hT = [sb(ph, f"hT{i}", [128, NDC, 128], BF16) for i in range(2)]
                        st = [sb(ph, f"st_{i}", [128, 4, 6], F32) for i in range(2)]
                        mv = [sb(ph, f"mv_{i}", [128, 4], F32) for i in range(2)]
                        T_Wo, T_Gt, T_Bt = (Buf(n) for n in ("Wo", "Gt", "Bt"))
                        T_tt, T_st, T_mv = ([Buf(f"{n}{i}") for i in range(3)] for n in ("tt", "st", "mv"))
                        T_xb = [Buf(f"xb{i}") for i in range(2)]
                        T_hb = [Buf(f"hb{i}") for i in range(2)]
                        T_hT = [Buf(f"hT{i}") for i in range(2)]
                        yps = [ps(ph, f"yps{i}", [128, 512]) for i in range(4)]
                        T_y = [Buf(f"yps{i}") for i in range(4)]
                        tps = [ps(ph, f"tps{i}", [128, 512]) for i in range(4)]
                        T_tp = [Buf(f"tps{i}") for i in range(4)]
                        for q in range(2):
                            S.dma("pool", WoB[:, :, 512 * q:512 * (q + 1)], w_o0_v[:, :, 1024 + 512 * q:1024 + 512 * (q + 1)], T_Wo,
                                  reads=[B_w], pwrites=[T_Wo])

                        def wo0(c, q):
                            return WoA[:, c, 512 * q:512 * (q + 1)] if q < 2 else WoB[:, c, 512 * (q - 2):512 * (q - 1)]
                        S.dma("sp", Gt[:], ln0g.partition_broadcast(128), T_Gt, reads=[B_w], writes=[T_Gt])
                        S.dma("sp", Bt[:], ln0b.partition_broadcast(128), T_Bt, reads=[B_w], writes=[T_Bt])
                        ln_block_loop(nc, S, NOWN, ogT, T_ogT, NOWN, wo0, [T_WoA, T_Wo], xown, B_xown, Gt, T_Gt, Bt, T_Bt,
                                      xb, T_xb, tt, T_tt, hb, T_hb, st, T_st, mv, T_mv, yps, T_y,
                                      H1_s, B_H1, lambda r0: r0, hT=hT, T_hT=T_hT, tps=tps, T_tp=T_tp, ident=ident,
                                      T_ident=T_ident, H1T_s=H1T_s, B_H1T=B_H1T, evac=evac)
                        S.barrier(release=[T_Wo, T_Gt, T_Bt] + T_xb + T_hb + T_hT)
            S.barrier()

        if upto >= 7:
            with ExitStack() as phE:
                gT = sb(phE, "gT", [128, NDC, 2048], BF16)
                T_gT = Buf("gT")
                with ExitStack() as ph:
                    h1T = sb(ph, "h1T", [128, NDC, NOWN], BF16)
                    T_h1T = Buf("h1T")
                    for q in range(4):
                        S.dma("sp", h1T[:, 4 * q:4 * q + 4, :], H1T_s[:, 4 * q:4 * q + 4, :], T_h1T, reads=[B_H1T], pwrites=[T_h1T])
                    Wc = [sb(ph, f"WcE{i}", [128, NDC, 256], BF16) for i in range(2)]
                    T_Wc = [Buf(f"WcE{i}") for i in range(2)]
                    Wg = sb(ph, "Wg", [128, 4, 512], BF16)
                    T_Wg = Buf("Wg")
                    psc_ = sb(ph, "pscale", [128, NDC], F32)
                    T_psc = Buf("pscale")
                    dT = sb(ph, "dT", [128, 4, 2048], BF16)
                    T_dT = Buf("dT")
                    szT = sb(ph, "szT", [128, 4, 2048], BF16)
                    T_szT = Buf("szT")
                    uu = [sb(ph, f"uu{i}", [128, SB], F32) for i in range(2)]
                    T_uu = [Buf(f"uu{i}") for i in range(2)]
                    wa = sb(ph, "wa", [128, SB], F32)
                    wb = sb(ph, "wb", [128, SB], F32)
                    T_wa, T_wb = Buf("wa"), Buf("wb")
                    pu = [ps(ph, f"puE{i}", [128, 1024]) for i in range(2)]
                    T_pu = [Buf(f"puE{i}") for i in range(2)]
                    pz = [ps(ph, f"pzE{i}", [128, 512]) for i in range(2)]
                    T_pz = [Buf(f"pzE{i}") for i in range(2)]
                    pe_ = [ps(ph, f"peE{i}", [128, 512]) for i in range(2)]
                    T_pe = [Buf(f"peE{i}") for i in range(2)]
                    S.dma("sp", psc_[:], p_scale, T_psc, reads=[B_w], writes=[T_psc])
                    chunks = []
                    for g in range(4):
                        chunks += [("u", g, 0), ("u", g, 1), ("z", g, 0), ("z", g, 1)]

                    def load_chunk(ci):
                        kind, g, hf = chunks[ci]
                        c0 = (0 if kind == "u" else 2048) + 512 * g + 256 * hf
                        S.dma("pool", Wc[ci % 2][:], pw_in_v[:, :, c0:c0 + 256], T_Wc[ci % 2],
                              reads=[B_w], writes=[T_Wc[ci % 2]])
                    load_chunk(0)
                    nu = 0
                    nz = 0
                    ne = 0
                    for ci, (kind, g, hf) in enumerate(chunks):
                        if ci + 1 < len(chunks):
                            load_chunk(ci + 1)
                        W, TW = Wc[ci % 2], T_Wc[ci % 2]
                        if kind == "u":
                            wwin = (2, 4, 8, 16)[g]
                            if hf == 0:
                                S.dma("pool", Wg[:], pw_grp[g].rearrange("(a p) e -> p a e", p=128), T_Wg,
                                      reads=[B_w], writes=[T_Wg])
                            for cc2 in range(2):
                                cc_ = 2 * hf + cc2
                                for j in range(4):
                                    pt, Tp = pu[nu % 2], T_pu[nu % 2]
                                    u_, Tu = uu[nu % 2], T_uu[nu % 2]
                                    nu += 1

                                    def mm(pe, pt=pt, W=W, cc2=cc2, j=j):
                                        ins = None
                                        for (c0, nn) in ((0, 512), (512, 16)):
                                            for dc in range(NDC):
                                                ins = pe.matmul(pt[:, c0:c0 + nn], W[:, dc, 128 * cc2:128 * cc2 + 128],
                                                                h1T[:, dc, SB * j + c0:SB * j + c0 + nn],
                                                                start=(dc == 0), stop=(dc == NDC - 1))
                                        return ins
                                    S.op("pe", mm, reads=[T_h1T, TW], writes=[Tp])
                                    S.op("act", lambda a, u_=u_, pt=pt: a.copy(u_[:], pt[:, 0:SB]), reads=[Tp], writes=[Tu])
                                    src, Tsrc = u_, Tu
                                    sh = 1
                                    bufs = [(wa, T_wa), (wb, T_wb)]
                                    bi = 0
                                    while sh < wwin:
                                        dst, Tdst = bufs[bi % 2]
                                        bi += 1
                                        S.op("dve", lambda v, dst=dst, src=src, sh=sh: v.tensor_tensor(
                                            dst[:, sh:SB], src[:, sh:SB], src[:, 0:SB - sh], ALU.add),
                                            reads=[Tsrc], writes=[Tdst])
                                        if sh == 1:
                                            pass
                                        src, Tsrc = dst, Tdst
                                        sh *= 2
                                    S.op("dve", lambda v, src=src, u_=u_, cc_=cc_, j=j, wwin=wwin: v.scalar_tensor_tensor(
                                        dT[:, cc_, 512 * j:512 * (j + 1)], src[:, 16:SB], 1.0 / wwin, u_[:, 16:SB],
                                        ALU.mult, ALU.subtract),
                                        reads=[Tsrc, Tu], pwrites=[T_dT])
                        else:
                            for cc2 in range(2):
                                cc_ = 2 * hf + cc2
                                for j in range(4):
                                    pt, Tp = pz[nz % 2], T_pz[nz % 2]
                                    nz += 1

                                    def mm(pe, pt=pt, W=W, cc2=cc2, j=j):
                                        ins = None
                                        for dc in range(NDC):
                                            ins = pe.matmul(pt[:, :], W[:, dc, 128 * cc2:128 * cc2 + 128],
                                                            h1T[:, dc, SB * j + 16:SB * j + SB],
                                                            start=(dc == 0), stop=(dc == NDC - 1))
                                        return ins
                                    S.op("pe", mm, reads=[T_h1T, TW], writes=[Tp])
                                    S.op("act", lambda a, pt=pt, cc_=cc_, j=j: a.activation(
                                        szT[:, cc_, 512 * j:512 * (j + 1)], pt[:, :], AF.Silu), reads=[Tp], pwrites=[T_szT])
                            for ec in (range(4) if hf == 1 else ()):
                                for j in range(4):
                                    pt, Tp = pe_[ne % 2], T_pe[ne % 2]
                                    ne += 1

                                    def mm(pe, pt=pt, ec=ec, j=j):
                                        ins = None
                                        for a in range(4):
                                            ins = pe.matmul(pt[:, :], Wg[:, a, 128 * ec:128 * ec + 128],
                                                            dT[:, a, 512 * j:512 * (j + 1)], start=(a == 0), stop=(a == 3))
                                        return ins
                                    S.op("pe", mm, reads=[T_dT, T_Wg], writes=[Tp])
                                    ch = 4 * g + ec
                                    S.op("dve", lambda v, pt=pt, ch=ch, ec=ec, j=j: v.scalar_tensor_tensor(
                                        gT[:, ch, 512 * j:512 * (j + 1)], pt[:, :], psc_[:, ch:ch + 1],
                                        szT[:, ec, 512 * j:512 * (j + 1)], ALU.mult, ALU.mult),
                                        reads=[Tp, T_psc, T_szT], pwrites=[T_gT])
                    S.barrier(release=[T_h1T, T_Wg, T_psc] + T_Wc)
                with ExitStack() as ph:
                    Wo = sb(ph, "Wo1", [128, NDC, D], BF16)
                    Gt = sb(ph, "Gt1", [128, D], F32)
                    Bt = sb(ph, "Bt1", [128, D], F32)
                    xb = [sb(ph, f"xb1{i}", [128, D], F32) for i in range(2)]
                    tt = [sb(ph, f"tt1_{i}", [128, D], F32) for i in range(2)]
                    hb = [sb(ph, f"hb1{i}", [128, D], F32) for i in range(2)]
                    st = [sb(ph, f"st1_{i}", [128, 4, 6], F32) for i in range(2)]
                    mv = [sb(ph, f"mv1_{i}", [128, 4], F32) for i in range(2)]
                    T_Wo, T_Gt, T_Bt = (Buf(n) for n in ("Wo1", "Gt1", "Bt1"))
                    T_tt, T_st, T_mv = ([Buf(f"{n}1{i}") for i in range(3)] for n in ("tt", "st", "mv"))
                    T_xb = [Buf(f"xb1{i}") for i in range(2)]
                    T_hb = [Buf(f"hb1{i}") for i in range(2)]
                    yps = [ps(ph, f"yps1{i}", [128, 512]) for i in range(4)]
                    T_y = [Buf(f"yps1{i}") for i in range(4)]
                    w_o1_v = pw_out.rearrange("(dc p) c -> p dc c", p=128)
                    for q in range(4):
                        S.dma("pool", Wo[:, :, 512 * q:512 * (q + 1)], w_o1_v[:, :, 512 * q:512 * (q + 1)], T_Wo,
                              reads=[B_w], pwrites=[T_Wo])
                    S.dma("sp", Gt[:], ln1g.partition_broadcast(128), T_Gt, reads=[B_w], writes=[T_Gt])
                    S.dma("sp", Bt[:], ln1b.partition_broadcast(128), T_Bt, reads=[B_w], writes=[T_Bt])
                    ln_block_loop(nc, S, 2048, gT, T_gT, 2048, (lambda c, q: Wo[:, c, 512 * q:512 * (q + 1)]), [T_Wo], H1_s, B_H1, Gt, T_Gt, Bt, T_Bt,
                                  xb, T_xb, tt, T_tt, hb, T_hb, st, T_st, mv, T_mv, yps, T_y,
                                  out, B_out, lambda r0: SB * (r0 // 512) + 16 + (r0 % 512))
                    S.barrier(release=[T_Wo, T_Gt, T_Bt] + T_xb + T_hb)
                S.barrier()
        S.barrier()
    return nc


def ln_block_loop(nc, S, ntok, aT, T_aT, acols, Wo, T_Wo, res, B_res, Gt, T_Gt, Bt, T_Bt,
                  xb, T_xb, tts, T_tts, hb, T_hb, sts, T_sts, mvs, T_mvs, yps, T_y,
                  dst, B_dst, res_row, hT=None, T_hT=None, tps=None, T_tp=None, ident=None, T_ident=None,
                  H1T_s=None, B_H1T=None, evac=None):
    nblk = (ntok + 127) // 128
    ntt = len(tts)

    def geom(blk):
        r0 = blk * 128
        return r0, min(128, ntok - r0)

    def load_x(blk):
        r0, n = geom(blk)
        rr0 = res_row(r0)
        S.dma("sp", xb[blk % 2][0:n, :], res[rr0:rr0 + n, :], T_xb[blk % 2], reads=[B_res], writes=[T_xb[blk % 2]])

    def stage_a(blk):
        r0, n = geom(blk)
        x_, Tx = xb[blk % 2], T_xb[blk % 2]
        tt, T_tt = tts[blk % ntt], T_tts[blk % ntt]
        st, T_st = sts[blk % 2], T_sts[blk % 2]
        mv, T_mv = mvs[blk % 2], T_mvs[blk % 2]
        if blk == 0:
            load_x(0)
        if blk + 1 < nblk:
            load_x(blk + 1)
        for q in range(4):
            def mm(pe, q=q):
                ins = None
                for c in range(NDC):
                    ins = pe.matmul(yps[q][0:n, :], aT[:, c, r0:r0 + n], Wo(c, q),
                                    start=(c == 0), stop=(c == NDC - 1))
                return ins
            S.op("pe", mm, reads=[T_aT] + list(T_Wo), writes=[T_y[q]])
            S.op("dve", lambda v, q=q: v.scalar_tensor_tensor(
                tt[0:n, 512 * q:512 * (q + 1)], x_[0:n, 512 * q:512 * (q + 1)], ALPHA, yps[q][0:n, :],
                ALU.mult, ALU.add), reads=[Tx, T_y[q]], pwrites=[T_tt])
            S.op("dve", lambda v, q=q: v.bn_stats(st[0:n, q, :], tt[0:n, 512 * q:512 * (q + 1)]),
                 reads=[T_tt], pwrites=[T_st])
        S.op("dve", lambda v: v.bn_aggr(mv[0:n, 0:2], st[0:n, :, :].rearrange("p a b -> p (a b)")),
             reads=[T_st], pwrites=[T_mv])
        S.op("act", lambda a: a.activation(mv[0:n, 2:3], mv[0:n, 1:2], AF.Sqrt, bias=LN_EPS, scale=1.0),
             reads=[T_mv], writes=[T_mv])
        S.op("dve", lambda v: v.reciprocal(mv[0:n, 2:3], mv[0:n, 2:3]), reads=[T_mv], writes=[T_mv])
        S.op("dve", lambda v: v.scalar_tensor_tensor(mv[0:n, 3:4], mv[0:n, 0:1], -1.0, mv[0:n, 2:3], ALU.mult, ALU.mult),
             reads=[T_mv], writes=[T_mv])
        h_, Th = hb[blk % 2], T_hb[blk % 2]
        S.op("act", lambda a: a.activation(h_[0:n, :], tt[0:n, :], AF.Identity, bias=mv[0:n, 3:4], scale=mv[0:n, 2:3]),
             reads=[T_mv, T_tt], writes=[Th])

    def stage_b(blk):
        r0, n = geom(blk)
        h_, Th = hb[blk % 2], T_hb[blk % 2]
        S.op("dve", lambda v: v.tensor_tensor(h_[0:n, :], h_[0:n, :], Gt[0:n, :], ALU.mult),
             reads=[Th, T_Gt], writes=[Th])
        S.op("pool", lambda g: g.tensor_tensor(h_[0:n, :], h_[0:n, :], Bt[0:n, :], ALU.add),
             reads=[Th, T_Bt], writes=[Th])
        S.dma("pool", dst[r0:r0 + n, :], h_[0:n, :], Th, reads=[Th], pwrites=[B_dst])

    def stage_c(blk):
        if hT is None:
            return
        r0, n = geom(blk)
        h_, Th = hb[blk % 2], T_hb[blk % 2]
        t_, Tt = hT[blk % 2], T_hT[blk % 2]
        for g4 in range(4):
            pt, Tp = tps[g4 % len(tps)], T_tp[g4 % len(tps)]

            def tr(pe, pt=pt, g4=g4):
                ins = None
                for q in range(4):
                    dc = 4 * g4 + q
                    ins = pe.transpose(pt[:, q * 128:q * 128 + n], h_[0:n, dc * 128:(dc + 1) * 128], ident[0:n, 0:n])
                return ins
            S.op("pe", tr, reads=[Th, T_ident], writes=[Tp])
            evac(t_[:, 4 * g4:4 * g4 + 4, 0:n], pt[:, :].rearrange("p (a b) -> p a b", a=4)[:, :, 0:n],
                 reads=[Tp], pwrites=[Tt], eng="act")
        S.dma("sp", H1T_s[:, :, r0:r0 + n], t_[:, :, 0:n], Tt, reads=[Tt], pwrites=[B_H1T])

    for i in range(nblk + 2):
        if 0 <= i - 2 < nblk:
            stage_c(i - 2)
        if 0 <= i - 1 < nblk:
            stage_b(i - 1)
        if i < nblk:
            stage_a(i)


def make_masks(s):
    p = np.arange(128)[:, None]
    i = np.arange(SB)[None, :]
    rowm = np.zeros((10, SB), np.float32)
    tri = np.zeros((128, 9, 128), np.float32)
    ht = np.zeros((128, 10, 16), np.float32)
    for r in range(9):
        M = np.where(128 * r - 112 + p <= 512 * s + i, 0.0, NEG).astype(np.float32)
        row = np.where((M == NEG).all(axis=0), NEG, 0.0).astype(np.float32)
        res = M - row[None, :]
        res[:, row == NEG] = 0.0
        rowm[r + 1] = row
        chk = res.copy()
        ht[:, r, :] = res[:, 0:16]
        chk[:, 0:16] = 0
        if r >= 1:
            tc = 16 + 128 * ((r - 1) % 4)
            tri[:, r, :] = res[:, tc:tc + 128]
            chk[:, tc:tc + 128] = 0
        assert not chk.any(), (s, r)
    pm = np.arange(16)[:, None]
    im = np.arange(16)[None, :]
    ht[0:16, 9, :] = np.where(pm <= 512 * s + im, 0.0, NEG)
    return rowm.reshape(1, -1), tri.reshape(128, -1), ht.reshape(128, -1)


def make_in_maps(inputs):
    x = np.asarray(inputs["x"], np.float32)
    meta = np.asarray(inputs["meta_tokens"], np.float32)
    shared = {
        "fox_w_in": np.ascontiguousarray(inputs["fox_w_in"], np.float32),
        "fox_b_f": np.ascontiguousarray(np.asarray(inputs["fox_b_f"], np.float32).reshape(NH, 1)),
        "fox_w_out": np.ascontiguousarray(inputs["fox_w_out"], np.float32),
        "ln0_g": np.ascontiguousarray(inputs["ln0_g"], np.float32),
        "ln0_b": np.ascontiguousarray(inputs["ln0_b"], np.float32),
        "pool_w_in": np.ascontiguousarray(inputs["pool_w_in"], np.float32),
        "pool_w_grp": np.ascontiguousarray(inputs["pool_w_grp"], np.float32),
        "pool_scale": np.ascontiguousarray(np.asarray(inputs["pool_scale"], np.float32).reshape(NDC, 128).T),
        "pool_w_out": np.ascontiguousarray(inputs["pool_w_out"], np.float32),
        "ln1_g": np.ascontiguousarray(inputs["ln1_g"], np.float32),
        "ln1_b": np.ascontiguousarray(inputs["ln1_b"], np.float32),
        "ident": np.eye(128, dtype=np.float32),
    }
    maps = []
    for c in range(8):
        b, s = c // 2, c % 2
        xall = np.concatenate([meta, x[b]], axis=0)
        idx = np.concatenate([np.arange(512 * (2 * j + s), 512 * (2 * j + s) + SB) for j in range(4)])
        ind = np.zeros((4, L), np.float32)
        for j in range(4):
            ind[j, :512 * (2 * j + s)] = 1.0
        rowm, tri, ht = make_masks(s)
        m = dict(shared)
        xown = np.ascontiguousarray(xall[idx])

        def fmajor(a):
            return np.ascontiguousarray(a.T.reshape(NDC, 128, a.shape[0]).transpose(1, 0, 2))
        m.update({"xallT": fmajor(xall), "xownT": fmajor(xown), "xown": xown,
                  "ind": ind, "rowmask": rowm, "tri": tri, "ht": ht})
        maps.append(m)
    return maps


_NC_CACHE = {}


def kernel(**inputs):
    if "nc" not in _NC_CACHE:
        _NC_CACHE["nc"] = build()
    nc = _NC_CACHE["nc"]
    maps = make_in_maps(inputs)
    res = run_bass_kernel_spmd(nc, maps, core_ids=list(range(8)))
    out = np.zeros((4, 4096, D), np.float32)
    for c in range(8):
        b, s = c // 2, c % 2
        o = np.asarray(res.results[c]["out"], np.float32)
        for j in range(4):
            m = 2 * j + s
            out[b, 512 * m:512 * (m + 1), :] = o[512 * j:512 * (j + 1), :]
    return out
```

```python
import numpy as np
from contextlib import ExitStack
import concourse.bass as bass
import concourse.mybir as mybir
from concourse.bass_utils import run_bass_kernel_spmd

F32 = mybir.dt.float32
BF16 = mybir.dt.bfloat16
AF = mybir.ActivationFunctionType
ALU = mybir.AluOpType
AX = mybir.AxisListType

D = 2048
NDC = 16
L = 4112
NOWN = 2112
SB = 528
NH = 16
NKB = 33
ALPHA = 4.0 ** 0.25
LN_EPS = 1e-5
QSCALE = 128.0 ** -0.5
NEG = -30000.0
FOXC = 8208


def kb_range(kb):
    return (0, 16) if kb == 0 else (16 + 128 * (kb - 1), 128)


class Sem:
    __slots__ = ("h", "i", "h_pool")

    def __init__(self, h, i):
        self.h = h
        self.i = i
        self.h_pool = False


class Buf:
    __slots__ = ("name", "w", "r", "war", "dsem", "dcnt")

    def __init__(self, name):
        self.name = name
        self.w = {}
        self.r = {}
        self.war = {}
        self.dsem = None
        self.dcnt = 0


class Eng:
    def __init__(self, h, sem):
        self.h = h
        self.sem = sem
        self.cnt = 0
        self.seen = {}


class Sched:
    def __init__(self, nc, es):
        self.nc = nc
        self.es = es
        self.nsem = 0
        self.sems = {}
        self.free_dsems = []
        self.all_dbufs = []
        self.E = {}
        for k, h in (("pe", nc.tensor), ("act", nc.scalar), ("dve", nc.vector),
                     ("pool", nc.gpsimd), ("sp", nc.sync)):
            self.E[k] = Eng(h, self.newsem("e_" + k))

    def newsem(self, name):
        h = self.es.enter_context(self.nc.semaphore(name))
        s = Sem(h, self.nsem)
        self.sems[self.nsem] = s
        self.nsem += 1
        return s

    def _merge(self, toks, d):
        for i, v in d.items():
            if toks.get(i, 0) < v:
                toks[i] = v

    def _wait(self, e, toks):
        for i, v in toks.items():
            if e.seen.get(i, 0) >= v:
                continue
            e.h.wait_ge(self.sems[i].h, v)
            e.seen[i] = v

    def _deps(self, reads, writes, pwrites):
        toks = {}
        for b in reads:
            self._merge(toks, b.w)
        for b in writes:
            self._merge(toks, b.w)
            self._merge(toks, b.r)
            self._merge(toks, b.war)
        for b in pwrites:
            self._merge(toks, b.r)
            self._merge(toks, b.war)
        return toks

    def _commit(self, tok, reads, writes, pwrites):
        i, v = tok
        for b in reads:
            if b.r.get(i, 0) < v:
                b.r[i] = v
        for b in list(writes) + list(pwrites):
            if b.w.get(i, 0) < v:
                b.w[i] = v
            if b.r:
                self._merge(b.war, b.r)
                b.r = {}

    def op(self, eng, fn, reads=(), writes=(), pwrites=()):
        e = self.E[eng]
        toks = self._deps(reads, writes, pwrites)
        if eng == "pe":
            toks.pop(e.sem.i, None)
        self._wait(e, toks)
        ins = fn(e.h)
        e.cnt += 1
        ins.then_inc(e.sem.h, 1)
        self._commit((e.sem.i, e.cnt), reads, writes, pwrites)

    def dma(self, q, out, in_, sbuf, reads=(), writes=(), pwrites=()):
        e = self.E[q]
        toks = self._deps(reads, writes, pwrites)
        self._wait(e, toks)
        if sbuf.dsem is None:
            if self.free_dsems and q != "pool":
                sbuf.dsem, sbuf.dcnt = self.free_dsems.pop()
            else:
                sbuf.dsem = self.newsem("d_" + sbuf.name)
                sbuf.dcnt = 0
            self.all_dbufs.append(sbuf)
        if q == "pool":
            sbuf.dsem.h_pool = True
        e.h.dma_start(out=out, in_=in_).then_inc(sbuf.dsem.h, 16)
        sbuf.dcnt += 16
        self._commit((sbuf.dsem.i, sbuf.dcnt), reads, writes, pwrites)

    def barrier(self, release=()):
        toks = {}
        for e in self.E.values():
            if e.cnt:
                toks[e.sem.i] = e.cnt
        for b in self.all_dbufs:
            if b.dsem is not None:
                toks[b.dsem.i] = max(toks.get(b.dsem.i, 0), b.dcnt)
        for e in self.E.values():
            self._wait(e, toks)
        for b in release:
            if b.dsem is not None:
                if not b.dsem.h_pool:
                    self.free_dsems.append((b.dsem, b.dcnt))
                self.all_dbufs.remove(b)
                b.dsem = None


def build(upto=99, dbg=False):
    nc = bass.Bass("TRN2", target_bir_lowering=False)

    def din(name, shape, dt=F32):
        return nc.dram_tensor(name, list(shape), dt, kind="ExternalInput").ap()

    def dscr(name, shape, dt, out=False):
        kind = "ExternalOutput" if (out and dbg) else "Internal"
        return nc.dram_tensor(name, list(shape), dt, kind=kind).ap()

    xallT = din("xallT", [128, NDC, L])
    xownT = din("xownT", [128, NDC, NOWN])
    xown = din("xown", [NOWN, D])
    w_in = din("fox_w_in", [D, FOXC])
    b_f = din("fox_b_f", [NH, 1])
    w_o0 = din("fox_w_out", [D, D])
    ln0g = din("ln0_g", [D])
    ln0b = din("ln0_b", [D])
    pw_in = din("pool_w_in", [D, 2 * D])
    pw_grp = din("pool_w_grp", [4, 512, 512])
    p_scale = din("pool_scale", [128, NDC])
    pw_out = din("pool_w_out", [D, D])
    ln1g = din("ln1_g", [D])
    ln1b = din("ln1_b", [D])
    ident_in = din("ident", [128, 128])
    spar_in = din("spar", [16, 1])
    rowm_in = din("rowmask", [1, 10 * SB])
    tri_in = din("tri", [128, 9 * 128])
    ht_in = din("ht", [128, 10 * 16])
    out = nc.dram_tensor("out", [2048, D], F32, kind="ExternalOutput").ap()

    Kt_s = dscr("Kt_s", [NH, 128, L], BF16, out=True)
    V_s = dscr("V_s", [NH, 128, NKB * 128], BF16, out=True)
    Q_s = dscr("Q_s", [NH, 128, NOWN], BF16, out=True)
    Z_s = dscr("Z_s", [NH, 128, NOWN], BF16, out=True)
    C_s = dscr("C_s", [NH, 3 * NOWN], BF16, out=True)
    CK_s = dscr("CK_s", [128, NKB * NH], F32, out=True)
    OG_s = dscr("OG_s", [128, NH * NOWN], BF16, out=True)
    H1_s = dscr("H1_s", [NOWN, D], F32, out=True)
    H1T_s = dscr("H1T_s", [128, NDC, NOWN], BF16, out=True)

    w_in_v = w_in.rearrange("(dc p) c -> p dc c", p=128)
    pw_in_v = pw_in.rearrange("(dc p) c -> p dc c", p=128)

    with ExitStack() as es:
        S = Sched(nc, es)

        def sb(stack, name, shape, dt):
            return stack.enter_context(nc.sbuf_tensor("s_" + name, list(shape), dt))

        def ps(stack, name, shape, dt=F32):
            return stack.enter_context(nc.psum_tensor("p_" + name, list(shape), dt))

        B_xall, B_xown, B_w = Buf("xall"), Buf("xown"), Buf("w")
        B_Kt, B_V, B_Q, B_Z, B_C, B_CK = (Buf(n) for n in ("Kt", "V", "Q", "Z", "C", "CK"))
        B_OG, B_H1, B_H1T, B_out = Buf("OG"), Buf("H1"), Buf("H1T"), Buf("out")

        ident = sb(es, "ident", [128, 128], F32)
        identb = sb(es, "identb", [128, 128], BF16)
        ones4 = sb(es, "ones4", [4, 128], BF16)
        onesf = sb(es, "onesf", [128, 128], F32)
        ckT = sb(es, "ckT", [128, NKB * NH], F32)
        base = sb(es, "base", [16, 4], F32)
        nbf = sb(es, "nbf", [16, 1], F32)
        T_ident, T_identb, T_ones4, T_onesf, T_ckT, T_base, T_nbf = (
            Buf(n) for n in ("ident", "identb", "ones4", "onesf", "ckT", "base", "nbf"))
        S.dma("sp", ident[:], ident_in, T_ident, reads=[B_w], writes=[T_ident])
        S.dma("pool", identb[:], ident_in, T_identb, reads=[B_w], writes=[T_identb])
        S.dma("sp", nbf[:], b_f, T_nbf, reads=[B_w], writes=[T_nbf])
        S.op("dve", lambda v: v.memset(ones4[:], 1.0), writes=[T_ones4])
        S.op("dve", lambda v: v.memset(onesf[:], 1.0), writes=[T_onesf])
        S.op("dve", lambda v: v.tensor_scalar(nbf[:], nbf[:], -1.0, None, ALU.mult),
             reads=[T_nbf], writes=[T_nbf])

        evac_rr = [0]

        def evac(out_ap, in_ap, reads, writes=(), pwrites=(), scale=None, eng=None):
            if eng is None:
                eng = "act" if (evac_rr[0] & 1) else "dve"
                evac_rr[0] += 1
            if eng == "act":
                if scale is None:
                    S.op("act", lambda a: a.copy(out_ap, in_ap), reads, writes, pwrites)
                else:
                    S.op("act", lambda a: a.mul(out_ap, in_ap, scale), reads, writes, pwrites)
            else:
                if scale is None:
                    S.op("dve", lambda v: v.tensor_copy(out_ap, in_ap), reads, writes, pwrites)
                else:
                    S.op("dve", lambda v: v.tensor_scalar(out_ap, in_ap, scale, None, ALU.mult),
                         reads, writes, pwrites)

        def load_transposed(stack, src, B_src, ntok, xT, T_xT, tag):
            with ExitStack() as ph:
                NXR = 4
                xrow = [sb(ph, f"xrow{tag}{i}", [128, D], F32) for i in range(NXR)]
                T_xrow = [Buf(f"xrow{tag}{i}") for i in range(NXR)]
                pst = [ps(ph, f"pst{tag}{i}", [128, 512]) for i in range(2)]
                T_pst = [Buf(f"pst{tag}{i}") for i in range(2)]
                nblk = (ntok + 127) // 128
                for blk in range(nblk):
                    r0 = blk * 128
                    n = min(128, ntok - r0)
                    xr, Tx = xrow[blk % NXR], T_xrow[blk % NXR]
                    S.dma("sp", xr[0:n, :], src[r0:r0 + n, :], Tx, reads=[B_src], writes=[Tx])
                    for g4 in range(4):
                        pt, Tp = pst[g4 % 2], T_pst[g4 % 2]

                        def tr(pe, pt=pt, g4=g4, xr=xr, n=n):
                            ins = None
                            for q in range(4):
                                dc = 4 * g4 + q
                                ins = pe.transpose(pt[:, q * 128:q * 128 + n],
                                                   xr[0:n, dc * 128:(dc + 1) * 128], ident[0:n, 0:n])
                            return ins
                        S.op("pe", tr, reads=[Tx, T_ident], writes=[Tp])
                        src_ap = pt[:, :].rearrange("p (a b) -> p a b", a=4)[:, :, 0:n]
                        evac(xT[:, 4 * g4:4 * g4 + 4, r0:r0 + n], src_ap, reads=[Tp], pwrites=[T_xT])
                S.barrier(release=T_xrow)

        def logf_from(stack, xT, T_xT, ntok, tiles, lf, T_lf, wf, T_wf, psf, T_psf):
            for (p0, n) in tiles:
                def mm(pe, p0=p0, n=n):
                    ins = None
                    for dc in range(NDC):
                        ins = pe.matmul(psf[0:16, 0:n], wf[:, dc, :], xT[:, dc, p0:p0 + n],
                                        start=(dc == 0), stop=(dc == NDC - 1))
                    return ins
                S.op("pe", mm, reads=list(T_xT) + [T_wf], writes=[T_psf])
                S.op("act", lambda a, p0=p0, n=n: a.activation(
                    lf[:, p0:p0 + n], psf[0:16, 0:n], AF.Exp, bias=nbf[:, 0:1], scale=-1.0),
                    reads=[T_psf, T_nbf], pwrites=[T_lf])
            S.op("act", lambda a: a.activation(lf[:, 0:ntok], lf[:, 0:ntok], AF.Ln, bias=1.0, scale=1.0),
                 reads=[T_lf], writes=[T_lf])
            S.op("dve", lambda v: v.tensor_scalar(lf[:, 0:ntok], lf[:, 0:ntok], -1.0, None, ALU.mult),
                 reads=[T_lf], writes=[T_lf])

        with ExitStack() as phA:
            xT = sb(phA, "xT", [128, NDC, L], BF16)
            T_xTt = [Buf(f"xT{t}") for t in range(9)]
            T_xT = T_xTt

            def xa_bufs(p0, n):
                return [T_xTt[t] for t in range(p0 // 512, (p0 + n - 1) // 512 + 1)]

            def load_x_tiles():
                for t in range(9):
                    p0, n = (512 * t, 512) if t < 8 else (4096, 16)
                    S.dma("pool", xT[:, :, p0:p0 + n], xallT[:, :, p0:p0 + n], T_xTt[t],
                          reads=[B_xall], writes=[T_xTt[t]])

            with ExitStack() as ph:
                Wc = [sb(ph, f"WcA{i}", [128, NDC, 512], BF16) for i in range(2)]
                T_Wc = [Buf(f"WcA{i}") for i in range(2)]
                KTs = [sb(ph, f"KTs{i}", [128, L], BF16) for i in range(1)]
                T_KTs = [Buf(f"KTs{i}") for i in range(1)]
                VS = sb(ph, "VS", [128, 4, NKB, 128], BF16)
                T_VS = Buf("VS")
                T_VSz = Buf("VSz")
                pk = [ps(ph, f"pkA{i}", [128, 512]) for i in range(4)]
                T_pk = [Buf(f"pkA{i}") for i in range(4)]
                VSf = VS[:].rearrange("p a b c -> p (a b c)")
                chunks = [("k", hg) for hg in range(4)] + [("v", hg) for hg in range(4)]
                if upto < 2:
                    chunks = []

                def load_chunk(ci):
                    kind, hg = chunks[ci]
                    c0 = (2048 if kind == "k" else 4096) + 512 * hg
                    S.dma("pool", Wc[ci % 2][:], w_in_v[:, :, c0:c0 + 512], T_Wc[ci % 2],
                          reads=[B_w], writes=[T_Wc[ci % 2]])
                if chunks:
                    load_chunk(0)
                load_x_tiles()
                pki = 0
                nks = 0
                for ci, (kind, hg) in enumerate(chunks):
                    if ci + 1 < len(chunks):
                        load_chunk(ci + 1)
                    W, TW = Wc[ci % 2], T_Wc[ci % 2]
                    if kind == "v" and hg == 0:
                        S.op("pool", lambda g: g.memset(VSf, 0.0), writes=[T_VS, T_VSz])
                    if kind == "k" and hg == 0:
                        for t in range(9):
                            p0, n = (512 * t, 512) if t < 8 else (4096, 16)
                            for hh in range(4):
                                pt, Tp = pk[pki % 4], T_pk[pki % 4]
                                pki += 1

                                def mm(pe, pt=pt, W=W, hh=hh, p0=p0, n=n):
                                    ins = None
                                    for dc in range(NDC):
                                        ins = pe.matmul(pt[:, 0:n], W[:, dc, 128 * hh:128 * hh + 128],
                                                        xT[:, dc, p0:p0 + n], start=(dc == 0), stop=(dc == NDC - 1))
                                    return ins
                                S.op("pe", mm, reads=xa_bufs(p0, n) + [TW], writes=[Tp])
                                evac(VSf[:, hh * L + p0:hh * L + p0 + n], pt[:, 0:n], reads=[Tp], pwrites=[T_VS])
                        for hh in range(4):
                            S.dma("sp", Kt_s[hh], VSf[:, hh * L:(hh + 1) * L], T_VS, reads=[T_VS], pwrites=[B_Kt])
                    elif kind == "k":
                        for hh in range(4):
                            h = 4 * hg + hh
                            kt, Tk = KTs[0], T_KTs[0]
                            nks += 1
                            for t in range(9):
                                p0, n = (512 * t, 512) if t < 8 else (4096, 16)
                                pt, Tp = pk[pki % 4], T_pk[pki % 4]
                                pki += 1

                                def mm(pe, pt=pt, W=W, hh=hh, p0=p0, n=n):
                                    ins = None
                                    for dc in range(NDC):
                                        ins = pe.matmul(pt[:, 0:n], W[:, dc, 128 * hh:128 * hh + 128],
                                                        xT[:, dc, p0:p0 + n], start=(dc == 0), stop=(dc == NDC - 1))
                                    return ins
                                S.op("pe", mm, reads=xa_bufs(p0, n) + [TW], writes=[Tp])
                                evac(kt[:, p0:p0 + n], pt[:, 0:n], reads=[Tp], pwrites=[Tk])
                            S.dma("sp", Kt_s[h], kt[:], Tk, reads=[Tk], pwrites=[B_Kt])
                    else:
                        for kb in range(NKB):
                            p0, n = kb_range(kb)
                            pt, Tp = pk[pki % 4], T_pk[pki % 4]
                            pki += 1

                            def mm(pe, pt=pt, W=W, p0=p0, n=n):
                                ins = None
                                for dc in range(NDC):
                                    ins = pe.matmul(pt[0:n, :], xT[:, dc, p0:p0 + n], W[:, dc, :],
                                                    start=(dc == 0), stop=(dc == NDC - 1))
                                return ins
                            S.op("pe", mm, reads=xa_bufs(p0, n) + [TW], writes=[Tp])
                            evac(VS[0:n, :, kb, :], pt[0:n, :].rearrange("p (a b) -> p a b", a=4),
                                 reads=[Tp, T_VSz], pwrites=[T_VS])
                        for hh in range(4):
                            S.dma("sp", V_s[4 * hg + hh], VS[:, hh].rearrange("p a b -> p (a b)"), T_VS,
                                  reads=[T_VS], pwrites=[B_V])
                S.barrier(release=T_Wc + T_KTs + [T_VS])
            with ExitStack() as ph:
                wf = sb(ph, "wf", [128, NDC, 16], BF16)
                lf = sb(ph, "lf", [16, L], F32)
                cc = sb(ph, "cc", [16, L], F32)
                tmp = sb(ph, "ctmp", [16, L], F32)
                ind = sb(ph, "ind", [16, L], F32)
                psf = ps(ph, "psf", [128, 512])
                psc = ps(ph, "psc", [128, 1024])
                T_wf, T_lf, T_cc, T_tmp, T_ind, T_psf, T_psc = (
                    Buf(n) for n in ("wf", "lf", "cc", "tmp", "ind", "psf", "psc"))
                S.dma("pool", wf[:], w_in_v[:, :, 8192:8208], T_wf, reads=[B_w], writes=[T_wf])
                tiles = [(512 * t, 512) for t in range(8)] + [(4096, 16)]
                logf_from(ph, xT, T_xT, L, tiles, lf, T_lf, wf, T_wf, psf, T_psf)
                S.op("dve", lambda v: v.memset(tmp[:], 1.0), writes=[T_tmp])
                S.op("dve", lambda v: v.tensor_tensor_scan(cc[:], tmp[:], lf[:], 0.0, ALU.mult, ALU.add),
                     reads=[T_tmp, T_lf], writes=[T_cc])
                S.dma("sp", ind[:, 0:1], spar_in, T_ind, reads=[B_w], writes=[T_ind])
                S.op("dve", lambda v: v.tensor_scalar(base[:, 0:1], cc[:, 511:512], ind[:, 0:1], None, ALU.mult),
                     reads=[T_cc, T_ind], writes=[T_base])
                for j in range(1, 4):
                    a, b = 1024 * j - 1, 1024 * j + 511
                    S.op("dve", lambda v, a=a, b=b: v.tensor_tensor(tmp[:, 0:1], cc[:, b:b + 1], cc[:, a:a + 1], ALU.subtract),
                         reads=[T_cc], writes=[T_tmp])
                    S.op("dve", lambda v, a=a, j=j: v.scalar_tensor_tensor(
                        base[:, j:j + 1], tmp[:, 0:1], ind[:, 0:1], cc[:, a:a + 1], ALU.mult, ALU.add),
                        reads=[T_tmp, T_ind, T_cc], writes=[T_base])
                def trc(pe):
                    ins = None
                    for kb in range(NKB):
                        p0, n = kb_range(kb)
                        ins = pe.transpose(psc[0:n, kb * 16:(kb + 1) * 16], cc[0:16, p0:p0 + n], ident[0:16, 0:16])
                    return ins
                S.op("pe", trc, reads=[T_cc, T_ident], writes=[T_psc])
                S.op("dve", lambda v: v.memset(ckT[:], 0.0), writes=[T_ckT])
                S.op("dve", lambda v: v.tensor_scalar(ckT[:, 16:NKB * NH], psc[:, 16:NKB * NH], -1.0, None, ALU.mult),
                     reads=[T_psc], pwrites=[T_ckT])
                S.op("dve", lambda v: v.tensor_scalar(ckT[0:16, 0:16], psc[0:16, 0:16], -1.0, None, ALU.mult),
                     reads=[T_psc], pwrites=[T_ckT])
                if dbg:
                    S.dma("sp", CK_s, ckT[:], T_ckT, reads=[T_ckT], pwrites=[B_CK])
                S.barrier(release=[T_wf, T_ind])

            S.barrier()

        if upto >= 3:
            with ExitStack() as phB:
                xTo = sb(phB, "xTo", [128, NDC, NOWN], BF16)
                T_xTo = [Buf(f"xTo{j}") for j in range(4)]

                def load_xo_tiles():
                    for j in range(4):
                        S.dma("pool", xTo[:, :, SB * j:SB * (j + 1)], xownT[:, :, SB * j:SB * (j + 1)], T_xTo[j],
                              reads=[B_xown], writes=[T_xTo[j]])
                with ExitStack() as ph:
                    Wc = [sb(ph, f"WcB{i}", [128, NDC, 512], BF16) for i in range(2)]
                    T_Wc = [Buf(f"WcB{i}") for i in range(2)]
                    QTs = [sb(ph, f"QTs{i}", [128, NOWN], BF16) for i in range(4)]
                    T_QTs = [Buf(f"QTs{i}") for i in range(4)]
                    pq = [ps(ph, f"pqB{i}", [128, 1024]) for i in range(3)]
                    T_pq = [Buf(f"pqB{i}") for i in range(3)]
                    chunks = [("q", hg) for hg in range(4)] + [("z", hg) for hg in range(4)]

                    def load_chunk(ci):
                        kind, hg = chunks[ci]
                        c0 = (0 if kind == "q" else 6144) + 512 * hg
                        S.dma("pool", Wc[ci % 2][:], w_in_v[:, :, c0:c0 + 512], T_Wc[ci % 2],
                              reads=[B_w], writes=[T_Wc[ci % 2]])
                    load_chunk(0)
                    load_xo_tiles()
                    pqi = 0
                    nqs = 0
                    for ci, (kind, hg) in enumerate(chunks):
                        if ci + 1 < len(chunks):
                            load_chunk(ci + 1)
                        W, TW = Wc[ci % 2], T_Wc[ci % 2]
                        for j in range(4):
                            for hh in range(4):
                                h = 4 * hg + hh
                                qt, Tq = QTs[hh], T_QTs[hh]
                                pt, Tp = pq[pqi % 3], T_pq[pqi % 3]
                                pqi += 1

                                def mm(pe, pt=pt, W=W, hh=hh, j=j):
                                    ins = None
                                    for (c0, o0, n) in ((0, SB * j, 512), (512, SB * j + 512, 16)):
                                        for dc in range(NDC):
                                            ins = pe.matmul(pt[:, c0:c0 + n], W[:, dc, 128 * hh:128 * hh + 128],
                                                            xTo[:, dc, o0:o0 + n], start=(dc == 0), stop=(dc == NDC - 1))
                                    return ins
                                S.op("pe", mm, reads=[T_xTo[j], TW], writes=[Tp])
                                if kind == "q":
                                    evac(qt[:, SB * j:SB * (j + 1)], pt[:, 0:SB], reads=[Tp], pwrites=[Tq],
                                         scale=QSCALE, eng="dve")
                                else:
                                    S.op("act", lambda a, qt=qt, pt=pt, j=j: a.activation(
                                        qt[:, SB * j:SB * (j + 1)], pt[:, 0:SB], AF.Silu),
                                        reads=[Tp], pwrites=[Tq])
                        for hh in range(4):
                            dst = Q_s if kind == "q" else Z_s
                            S.dma("sp", dst[4 * hg + hh], QTs[hh][:], T_QTs[hh], reads=[T_QTs[hh]],
                                  pwrites=[B_Q if kind == "q" else B_Z])
                        if ci == 0:
                            wf = sb(ph, "wfb", [128, NDC, 16], BF16)
                            lf = sb(ph, "lfb", [16, NOWN], F32)
                            cc = sb(ph, "ccb", [16, NOWN], F32)
                            t1 = sb(ph, "t1b", [16, NOWN], F32)
                            c3 = sb(ph, "c3b", [16, 3, NOWN], BF16)
                            psf = ps(ph, "psfb", [128, 512])
                            T_wf, T_lf, T_cc, T_t1, T_c3, T_psf = (Buf(n) for n in ("wfb", "lfb", "ccb", "t1b", "c3b", "psfb"))
                            S.dma("pool", wf[:], w_in_v[:, :, 8192:8208], T_wf, reads=[B_w], writes=[T_wf])
                            tiles = []
                            for j in range(4):
                                tiles += [(SB * j, 512), (SB * j + 512, 16)]
                            logf_from(ph, xTo, T_xTo, NOWN, tiles, lf, T_lf, wf, T_wf, psf, T_psf)
                            S.op("dve", lambda v: v.memset(t1[:], 1.0), writes=[T_t1])
                            for j in range(4):
                                S.op("dve", lambda v, j=j: v.tensor_tensor_scan(
                                    cc[:, SB * j:SB * (j + 1)], t1[:, SB * j:SB * (j + 1)], lf[:, SB * j:SB * (j + 1)],
                                    base[:, j:j + 1], ALU.mult, ALU.add),
                                    reads=[T_t1, T_lf, T_base], pwrites=[T_cc])
                            S.op("dve", lambda v: v.tensor_copy(c3[:, 0, :], cc[:]), reads=[T_cc], pwrites=[T_c3])
                            S.op("dve", lambda v: v.tensor_copy(t1[:], c3[:, 0, :]), reads=[T_c3], writes=[T_t1])
                            S.op("dve", lambda v: v.tensor_tensor(cc[:], cc[:], t1[:], ALU.subtract), reads=[T_t1, T_cc], writes=[T_cc])
                            S.op("dve", lambda v: v.tensor_copy(c3[:, 1, :], cc[:]), reads=[T_cc], pwrites=[T_c3])
                            S.op("dve", lambda v: v.tensor_copy(t1[:], c3[:, 1, :]), reads=[T_c3], writes=[T_t1])
                            S.op("dve", lambda v: v.tensor_tensor(cc[:], cc[:], t1[:], ALU.subtract), reads=[T_t1, T_cc], writes=[T_cc])
                            S.op("dve", lambda v: v.tensor_copy(c3[:, 2, :], cc[:]), reads=[T_cc], pwrites=[T_c3])
                            S.dma("sp", C_s, c3[:].rearrange("p a b -> p (a b)"), T_c3, reads=[T_c3], pwrites=[B_C])
                    S.barrier(release=T_Wc + T_QTs + [T_wf, T_c3])
                S.barrier()

        if upto >= 4:
            with ExitStack() as phC:
                ogT = sb(phC, "ogT", [128, NH, NOWN], BF16)
                T_ogT = Buf("ogT")
                WoA = sb(phC, "WoA", [128, NDC, 1024], BF16)
                T_WoA = Buf("WoA")
                w_o0_v = w_o0.rearrange("(dc p) c -> p dc c", p=128)
                with ExitStack() as ph:
                    KT = [sb(ph, f"KT{i}", [128, L], BF16) for i in range(2)]
                    VV = [sb(ph, f"VV{i}", [128, NKB, 128], BF16) for i in range(2)]
                    QT = [sb(ph, f"QT{i}", [128, NOWN], BF16) for i in range(2)]
                    ZT = [sb(ph, f"ZT{i}", [128, NOWN], BF16) for i in range(2)]
                    RR = [sb(ph, f"RR{i}", [128, 10, SB], BF16) for i in range(2)]
                    onesb = sb(ph, "onesb", [128, 128], BF16)
                    T_onesb = Buf("onesb")
                    S.op("pool", lambda g: g.memset(onesb[:], 1.0), writes=[T_onesb])
                    TRI = sb(ph, "TRI", [128, 9, 128], BF16)
                    HT = sb(ph, "HT", [128, 10, 16], BF16)
                    PT = [sb(ph, f"PT{i}", [128, SB], BF16) for i in range(3)]
                    Pacc = [sb(ph, f"Pacc{i}", [128, SB], F32) for i in range(2)]
                    rL = sb(ph, "rL", [128, SB], F32)
                    T_KT = [Buf(f"KT{i}") for i in range(2)]
                    T_VV = [Buf(f"VV{i}") for i in range(2)]
                    T_QT = [Buf(f"QT{i}") for i in range(2)]
                    T_ZT = [Buf(f"ZT{i}") for i in range(2)]
                    T_RR = [Buf(f"RR{i}") for i in range(2)]
                    T_RRm = [Buf(f"RRm{i}") for i in range(2)]
                    T_TRI, T_HT, T_rL = Buf("TRI"), Buf("HT"), Buf("rL")
                    T_Pacc = [Buf(f"Pacc{i}") for i in range(2)]
                    T_PT = [Buf(f"PT{i}") for i in range(3)]
                    Sps = [ps(ph, f"Sps{i}", [128, 1024]) for i in range(2)]
                    Ops = [ps(ph, f"Ops{i}", [128, 1024]) for i in range(2)]
                    T_S = [Buf(f"Sps{i}") for i in range(2)]
                    T_O = [Buf(f"Ops{i}") for i in range(2)]

                    S.dma("pool", TRI[:].rearrange("p a b -> p (a b)"), tri_in, T_TRI, reads=[B_w], writes=[T_TRI])
                    S.dma("pool", HT[:].rearrange("p a b -> p (a b)"), ht_in, T_HT, reads=[B_w], writes=[T_HT])
                    for i in range(2):
                        S.op("pool", lambda g, i=i: g.memset(RR[i][:].rearrange("p a b -> p (a b)"), 0.0), writes=[T_RR[i]])
                        S.dma("pool", RR[i][3:4, :, :].rearrange("p a b -> p (a b)"), rowm_in, T_RRm[i],
                              reads=[B_w], writes=[T_RR[i]])
                    heads = list(range(NH)) if upto >= 5 else [0]
                    if upto >= 6:
                        for q in range(2):
                            S.dma("pool", WoA[:, :, 512 * q:512 * (q + 1)], w_o0_v[:, :, 512 * q:512 * (q + 1)], T_WoA,
                                  reads=[B_w], pwrites=[T_WoA])

                    def load_head(hi):
                        h = heads[hi]
                        i = hi % 2
                        S.dma("sp", KT[i][:], Kt_s[h], T_KT[i], reads=[B_Kt], writes=[T_KT[i]])
                        S.dma("sp", VV[i][:].rearrange("p a b -> p (a b)"), V_s[h], T_VV[i], reads=[B_V], writes=[T_VV[i]])
                        S.dma("sp", QT[i][:], Q_s[h], T_QT[i], reads=[B_Q], writes=[T_QT[i]])
                        S.dma("sp", ZT[i][:], Z_s[h], T_ZT[i], reads=[B_Z], writes=[T_ZT[i]])

                    units = [(hi, j) for hi in range(len(heads)) for j in range(4)]
                    items = []
                    for u, (hi, j) in enumerate(units):
                        nkb = 8 * j + 9
                        order = list(range(1, nkb)) + [0]
                        for idx, kb in enumerate(order):
                            items.append(("p", u, kb, idx, len(order)))
                            if idx == 1 and u > 0:
                                items.append(("L", u - 1, 0, 0, 0))
                    items.append(("L", len(units) - 1, 0, 0, 0))
                    unit_rr = {}

                    def unit_setup(u):
                        hi, j = units[u]
                        if j == 0 and hi == 0:
                            load_head(0)
                        if j == 1 and hi + 1 < len(heads):
                            load_head(hi + 1)
                        h = heads[hi]
                        rr, Trr = RR[u % 2], T_RR[u % 2]
                        csrc = C_s[h:h + 1, :].rearrange("o (a b) -> (o a) b", a=3)[:, SB * j:SB * (j + 1)]
                        S.dma("sp", rr[0:3, :, :], csrc.unsqueeze(1).broadcast_to([3, 10, SB]), Trr,
                              reads=[B_C], pwrites=[Trr])
                        unit_rr[u] = (rr, Trr)

                    def produce(g):
                        kind, u, kb, idx, nord = items[g]
                        hi, j = units[u]
                        i = hi % 2
                        sp_, Ts = Sps[g % 2], T_S[g % 2]
                        if kind == "L":
                            pacc, Tpa = Pacc[u % 2], T_Pacc[u % 2]

                            def lm(pe):
                                ins = None
                                for (c0, nn) in ((0, 512), (512, 16)):
                                    ins = pe.matmul(sp_[:, c0:c0 + nn], onesf[:, :], pacc[:, c0:c0 + nn], start=True, stop=True)
                                return ins
                            S.op("pe", lm, reads=[Tpa, T_onesf], writes=[Ts])
                            return
                        if idx == 0:
                            unit_setup(u)
                        rr, Trr = unit_rr[u]
                        kt, qt = KT[i], QT[i]
                        q0 = SB * j
                        p0, n = kb_range(kb)
                        r = kb - 8 * j
                        v = r + 1 if (kb >= 1 and r >= 0) else 0

                        def mm(pe):
                            ins = None
                            for (c0, nn) in ((0, 512), (512, 16)):
                                pe.matmul(sp_[0:n, c0:c0 + nn], kt[:, p0:p0 + n], qt[:, q0 + c0:q0 + c0 + nn],
                                          start=True, stop=False)
                            masks = []
                            if kb == 0 and j == 0:
                                masks.append((0, 16, identb[0:16, 0:16], HT[0:16, 9, 0:16]))
                            if kb >= 1 and r >= 0:
                                masks.append((0, 16, identb[:, :], HT[:, r, 0:16]))
                                if r >= 1:
                                    tc = 16 + 128 * ((r - 1) % 4)
                                    if tc + 128 <= 512:
                                        masks.append((tc, 128, identb[:, :], TRI[:, r, 0:128]))
                                    else:
                                        masks.append((tc, 512 - tc, identb[:, :], TRI[:, r, 0:512 - tc]))
                                        masks.append((512, tc + 128 - 512, identb[:, :], TRI[:, r, 512 - tc:128]))
                            for (c0, nn) in ((0, 512), (512, 16)):
                                last = not any(((mc0 < 512) == (c0 < 512)) for (mc0, _, _, _) in masks)
                                ins = pe.matmul(sp_[0:n, c0:c0 + nn], onesb[:, 0:n], rr[:, v, c0:c0 + nn],
                                                start=False, stop=last)
                            for mi, (mc0, mn, lh, rh) in enumerate(masks):
                                later_same_bank = any(((m2[0] < 512) == (mc0 < 512)) for m2 in masks[mi + 1:])
                                ins = pe.matmul(sp_[0:n, mc0:mc0 + mn], lh, rh, start=False, stop=not later_same_bank,
                                                skip_group_check=True)
                            return ins
                        S.op("pe", mm, reads=[T_KT[i], T_QT[i], Trr, T_onesb, T_identb, T_TRI, T_HT], writes=[Ts])

                    npair = [0]
                    pt_of = {}

                    def consume(g, part):
                        kind, u, kb, idx, nord = items[g]
                        hi, j = units[u]
                        i = hi % 2
                        h = heads[hi]
                        q0 = SB * j
                        sp_, Ts = Sps[g % 2], T_S[g % 2]
                        ops_, To = Ops[u % 2], T_O[u % 2]
                        pacc, Tpa = Pacc[u % 2], T_Pacc[u % 2]
                        if kind == "L":
                            zt = ZT[i]
                            if part == 0:
                                S.op("act", lambda a: a.activation(rL[:], sp_[:, 0:SB], AF.Ln), reads=[Ts], writes=[T_rL])
                                S.op("act", lambda a: a.activation(rL[:], rL[:], AF.Exp, scale=-1.0), reads=[T_rL], writes=[T_rL])
                                return
                            S.op("dve", lambda v: v.tensor_tensor(rL[:], rL[:], zt[:, q0:q0 + SB], ALU.mult),
                                 reads=[T_rL, T_ZT[i]], writes=[T_rL])
                            S.op("dve", lambda v: v.tensor_tensor(ogT[:, h, q0:q0 + SB], ops_[:, 0:SB], rL[:], ALU.mult),
                                 reads=[To, T_rL], pwrites=[T_ogT])
                            return
                        p0, n = kb_range(kb)
                        vv = VV[i]
                        if part == 0:
                            pt_of[g] = (PT[npair[0] % 3], T_PT[npair[0] % 3])
                            npair[0] += 1
                        pt_, Tp = pt_of[g]
                        if part == 0:
                            S.op("act", lambda a: a.activation(
                                pt_[0:n, :], sp_[0:n, 0:SB], AF.Exp, bias=ckT[0:n, kb * 16 + h:kb * 16 + h + 1], scale=1.0),
                                reads=[Ts, T_ckT], writes=[Tp])
                            return

                        def pv(pe):
                            ins = None
                            for (c0, nn) in ((0, 512), (512, 16)):
                                ins = pe.matmul(ops_[:, c0:c0 + nn], vv[0:n, kb, :], pt_[0:n, c0:c0 + nn],
                                                start=(idx == 0), stop=(idx == nord - 1))
                            return ins
                        S.op("pe", pv, reads=[Tp, T_VV[i]], writes=[To])
                        if idx == 0:
                            S.op("dve", lambda v: v.tensor_copy(pacc[:, :], pt_[:, :]), reads=[Tp], writes=[Tpa])
                        else:
                            S.op("dve", lambda v: v.tensor_tensor(pacc[0:n, :], pacc[0:n, :], pt_[0:n, :], ALU.add),
                                 reads=[Tp, Tpa], writes=[Tpa])

                    nit = len(items)
                    produce(0)
                    produce(1)
                    for g in range(nit):
                        if g == nit - 1:
                            produce(g)
                        consume(g, 0)
                        if g + 2 < nit - 1:
                            produce(g + 2)
                        consume(g, 1)
                    if dbg:
                        S.dma("sp", OG_s, ogT[:].rearrange("p a b -> p (a b)"), T_ogT, reads=[T_ogT], pwrites=[B_OG])
                    S.barrier(release=T_KT + T_VV + T_QT + T_ZT + T_RR + [T_TRI, T_HT])

                if upto >= 6:
                    with ExitStack() as ph:
                        WoB = sb(ph, "WoB", [128, NDC, 1024], BF16)
                        Gt = sb(ph, "Gt", [128, D], F32)
                        Bt = sb(ph, "Bt", [128, D], F32)
                        xb = [sb(ph, f"xb{i}", [128, D], F32) for i in range(2)]
                        tt = [sb(ph, f"tt_{i}", [128, D], F32) for i in range(2)]
                        hb = [sb(ph, f"hb{i}", [128, D], F32) for i in range(2)]
                        hT = [sb(ph, f"hT{i}", [128, NDC, 128], BF16) for i in range(2)]
                        st = [sb(ph, f"st_{i}", [128, 4, 6], F32) for i in range(2)]
                        mv = [sb(ph, f"mv_{i}", [128, 4], F32) for i in range(2)]
                        T_Wo, T_Gt, T_Bt = (Buf(n) for n in ("Wo", "Gt", "Bt"))
                        T_tt, T_st, T_mv = ([Buf(f"{n}{i}") for i in range(3)] for n in ("tt", "st", "mv"))
                        T_xb = [Buf(f"xb{i}") for i in range(2)]
                        T_hb = [Buf(f"hb{i}") for i in range(2)]
                        T_hT = [Buf(f"hT{i}") for i in range(2)]
                        yps = [ps(ph, f"yps{i}", [128, 512]) for i in range(4)]
                        T_y = [Buf(f"yps{i}") for i in range(4)]
                        tps = [ps(ph, f"tps{i}", [128, 512]) for i in range(4)]
                        T_tp = [Buf(f"tps{i}") for i in range(4)]
                        for q in range(2):
                            S.dma("pool", WoB[:, :, 512 * q:512 * (q + 1)], w_o0_v[:, :, 1024 + 512 * q:1024 + 512 * (q + 1)], T_Wo,
                                  reads=[B_w], pwrites=[T_Wo])

                        def wo0(c, q):
                            return WoA[:, c, 512 * q:512 * (q + 1)] if q < 2 else WoB[:, c, 512 * (q - 2):512 * (q - 1)]
                        S.dma("sp", Gt[:], ln0g.partition_broadcast(128), T_Gt, reads=[B_w], writes=[T_Gt])
                        S.dma("sp", Bt[:], ln0b.partition_broadcast(128), T_Bt, reads=[B_w], writes=[T_Bt])
                        ln_block_loop(nc, S, NOWN, ogT, T_ogT, NOWN, wo0, [T_WoA, T_Wo], xown, B_xown, Gt, T_Gt, Bt, T_Bt,
                                      xb, T_xb, tt, T_tt, hb, T_hb, st, T_st, mv, T_mv, yps, T_y,
                                      H1_s, B_H1, lambda r0: r0, hT=hT, T_hT=T_hT, tps=tps, T_tp=T_tp, ident=ident,
                                      T_ident=T_ident, H1T_s=H1T_s, B_H1T=B_H1T, evac=evac)
                        S.barrier(release=[T_Wo, T_Gt, T_Bt] + T_xb + T_hb + T_hT)
            S.barrier()

        if upto >= 7:
            with ExitStack() as phE:
                gT = sb(phE, "gT", [128, NDC, 2048], BF16)
                T_gT = Buf("gT")
                with ExitStack() as ph:
                    h1T = sb(ph, "h1T", [128, NDC, NOWN], BF16)
                    T_h1T = Buf("h1T")
                    for q in range(4):
                        S.dma("sp", h1T[:, 4 * q:4 * q + 4, :], H1T_s[:, 4 * q:4 * q + 4, :], T_h1T, reads=[B_H1T], pwrites=[T_h1T])
                    Wc = [sb(ph, f"WcE{i}", [128, NDC, 256], BF16) for i in range(2)]
                    T_Wc = [Buf(f"WcE{i}") for i in range(2)]
                    Wg = sb(ph, "Wg", [128, 4, 512], BF16)
                    T_Wg = Buf("Wg")
                    psc_ = sb(ph, "pscale", [128, NDC], F32)
                    T_psc = Buf("pscale")
                    dT = sb(ph, "dT", [128, 4, 2048], BF16)
                    T_dT = Buf("dT")
                    szT = sb(ph, "szT", [128, 4, 2048], BF16)
                    T_szT = Buf("szT")
                    uu = [sb(ph, f"uu{i}", [128, SB], F32) for i in range(2)]
                    T_uu = [Buf(f"uu{i}") for i in range(2)]
                    wa = sb(ph, "wa", [128, SB], F32)
                    wb = sb(ph, "wb", [128, SB], F32)
                    T_wa, T_wb = Buf("wa"), Buf("wb")
                    pu = [ps(ph, f"puE{i}", [128, 1024]) for i in range(2)]
                    T_pu = [Buf(f"puE{i}") for i in range(2)]
                    pz = [ps(ph, f"pzE{i}", [128, 512]) for i in range(2)]
                    T_pz = [Buf(f"pzE{i}") for i in range(2)]
                    pe_ = [ps(ph, f"peE{i}", [128, 512]) for i in range(2)]
                    T_pe = [Buf(f"peE{i}") for i in range(2)]
                    S.dma("sp", psc_[:], p_scale, T_psc, reads=[B_w], writes=[T_psc])
                    chunks = []
                    for g in range(4):
                        chunks += [("u", g, 0), ("u", g, 1), ("z", g, 0), ("z", g, 1)]

                    def load_chunk(ci):
                        kind, g, hf = chunks[ci]
                        c0 = (0 if kind == "u" else 2048) + 512 * g + 256 * hf
                        S.dma("pool", Wc[ci % 2][:], pw_in_v[:, :, c0:c0 + 256], T_Wc[ci % 2],
                              reads=[B_w], writes=[T_Wc[ci % 2]])
                    load_chunk(0)
                    nu = 0
                    nz = 0
                    ne = 0
                    for ci, (kind, g, hf) in enumerate(chunks):
                        if ci + 1 < len(chunks):
                            load_chunk(ci + 1)
                        W, TW = Wc[ci % 2], T_Wc[ci % 2]
                        if kind == "u":
                            wwin = (2, 4, 8, 16)[g]
                            if hf == 0:
                                S.dma("pool", Wg[:], pw_grp[g].rearrange("(a p) e -> p a e", p=128), T_Wg,
                                      reads=[B_w], writes=[T_Wg])
                            for cc2 in range(2):
                                cc_ = 2 * hf + cc2
                                for j in range(4):
                                    pt, Tp = pu[nu % 2], T_pu[nu % 2]
                                    u_, Tu = uu[nu % 2], T_uu[nu % 2]
                                    nu += 1

                                    def mm(pe, pt=pt, W=W, cc2=cc2, j=j):
                                        ins = None
                                        for (c0, nn) in ((0, 512), (512, 16)):
                                            for dc in range(NDC):
                                                ins = pe.matmul(pt[:, c0:c0 + nn], W[:, dc, 128 * cc2:128 * cc2 + 128],
                                                                h1T[:, dc, SB * j + c0:SB * j + c0 + nn],
                                                                start=(dc == 0), stop=(dc == NDC - 1))
                                        return ins
                                    S.op("pe", mm, reads=[T_h1T, TW], writes=[Tp])
                                    S.op("act", lambda a, u_=u_, pt=pt: a.copy(u_[:], pt[:, 0:SB]), reads=[Tp], writes=[Tu])
                                    src, Tsrc = u_, Tu
                                    sh = 1
                                    bufs = [(wa, T_wa), (wb, T_wb)]
                                    bi = 0
                                    while sh < wwin:
                                        dst, Tdst = bufs[bi % 2]
                                        bi += 1
                                        S.op("dve", lambda v, dst=dst, src=src, sh=sh: v.tensor_tensor(
                                            dst[:, sh:SB], src[:, sh:SB], src[:, 0:SB - sh], ALU.add),
                                            reads=[Tsrc], writes=[Tdst])
                                        if sh == 1:
                                            pass
                                        src, Tsrc = dst, Tdst
                                        sh *= 2
                                    S.op("dve", lambda v, src=src, u_=u_, cc_=cc_, j=j, wwin=wwin: v.scalar_tensor_tensor(
                                        dT[:, cc_, 512 * j:512 * (j + 1)], src[:, 16:SB], 1.0 / wwin, u_[:, 16:SB],
                                        ALU.mult, ALU.subtract),
                                        reads=[Tsrc, Tu], pwrites=[T_dT])
                        else:
                            for cc2 in range(2):
                                cc_ = 2 * hf + cc2
                                for j in range(4):
                                    pt, Tp = pz[nz % 2], T_pz[nz % 2]
                                    nz += 1

                                    def mm(pe, pt=pt, W=W, cc2=cc2, j=j):
                                        ins = None
                                        for dc in range(NDC):
                                            ins = pe.matmul(pt[:, :], W[:, dc, 128 * cc2:128 * cc2 + 128],
                                                            h1T[:, dc, SB * j + 16:SB * j + SB],
                                                            start=(dc == 0), stop=(dc == NDC - 1))
                                        return ins
                                    S.op("pe", mm, reads=[T_h1T, TW], writes=[Tp])
                                    S.op("act", lambda a, pt=pt, cc_=cc_, j=j: a.activation(
                                        szT[:, cc_, 512 * j:512 * (j + 1)], pt[:, :], AF.Silu), reads=[Tp], pwrites=[T_szT])
                            for ec in (range(4) if hf == 1 else ()):
                                for j in range(4):
                                    pt, Tp = pe_[ne % 2], T_pe[ne % 2]
                                    ne += 1

                                    def mm(pe, pt=pt, ec=ec, j=j):
                                        ins = None
                                        for a in range(4):
                                            ins = pe.matmul(pt[:, :], Wg[:, a, 128 * ec:128 * ec + 128],
                                                            dT[:, a, 512 * j:512 * (j + 1)], start=(a == 0), stop=(a == 3))
                                        return ins
                                    S.op("pe", mm, reads=[T_dT, T_Wg], writes=[Tp])
                                    ch = 4 * g + ec
                                    S.op("dve", lambda v, pt=pt, ch=ch, ec=ec, j=j: v.scalar_tensor_tensor(
                                        gT[:, ch, 512 * j:512 * (j + 1)], pt[:, :], psc_[:, ch:ch + 1],
                                        szT[:, ec, 512 * j:512 * (j + 1)], ALU.mult, ALU.mult),
                                        reads=[Tp, T_psc, T_szT], pwrites=[T_gT])
                    S.barrier(release=[T_h1T, T_Wg, T_psc] + T_Wc)
                with ExitStack() as ph:
                    Wo = sb(ph, "Wo1", [128, NDC, D], BF16)
                    Gt = sb(ph, "Gt1", [128, D], F32)
                    Bt = sb(ph, "Bt1", [128, D], F32)
                    xb = [sb(ph, f"xb1{i}", [128, D], F32) for i in range(2)]
                    tt = [sb(ph, f"tt1_{i}", [128, D], F32) for i in range(2)]
                    hb = [sb(ph, f"hb1{i}", [128, D], F32) for i in range(2)]
                    st = [sb(ph, f"st1_{i}", [128, 4, 6], F32) for i in range(2)]
                    mv = [sb(ph, f"mv1_{i}", [128, 4], F32) for i in range(2)]
                    T_Wo, T_Gt, T_Bt = (Buf(n) for n in ("Wo1", "Gt1", "Bt1"))
                    T_tt, T_st, T_mv = ([Buf(f"{n}1{i}") for i in range(3)] for n in ("tt", "st", "mv"))
                    T_xb = [Buf(f"xb1{i}") for i in range(2)]
                    T_hb = [Buf(f"hb1{i}") for i in range(2)]
                    yps = [ps(ph, f"yps1{i}", [128, 512]) for i in range(4)]
                    T_y = [Buf(f"yps1{i}") for i in range(4)]
                    w_o1_v = pw_out.rearrange("(dc p) c -> p dc c", p=128)
                    for q in range(4):
                        S.dma("pool", Wo[:, :, 512 * q:512 * (q + 1)], w_o1_v[:, :, 512 * q:512 * (q + 1)], T_Wo,
                              reads=[B_w], pwrites=[T_Wo])
                    S.dma("sp", Gt[:], ln1g.partition_broadcast(128), T_Gt, reads=[B_w], writes=[T_Gt])
                    S.dma("sp", Bt[:], ln1b.partition_broadcast(128), T_Bt, reads=[B_w], writes=[T_Bt])
                    ln_block_loop(nc, S, 2048, gT, T_gT, 2048, (lambda c, q: Wo[:, c, 512 * q:512 * (q + 1)]), [T_Wo], H1_s, B_H1, Gt, T_Gt, Bt, T_Bt,
                                  xb, T_xb, tt, T_tt, hb, T_hb, st, T_st, mv, T_mv, yps, T_y,
                                  out, B_out, lambda r0: SB * (r0 // 512) + 16 + (r0 % 512))
                    S.barrier(release=[T_Wo, T_Gt, T_Bt] + T_xb + T_hb)
                S.barrier()
        S.barrier()
    return nc


def ln_block_loop(nc, S, ntok, aT, T_aT, acols, Wo, T_Wo, res, B_res, Gt, T_Gt, Bt, T_Bt,
                  xb, T_xb, tts, T_tts, hb, T_hb, sts, T_sts, mvs, T_mvs, yps, T_y,
                  dst, B_dst, res_row, hT=None, T_hT=None, tps=None, T_tp=None, ident=None, T_ident=None,
                  H1T_s=None, B_H1T=None, evac=None):
    nblk = (ntok + 127) // 128
    ntt = len(tts)

    def geom(blk):
        r0 = blk * 128
        return r0, min(128, ntok - r0)

    def load_x(blk):
        r0, n = geom(blk)
        rr0 = res_row(r0)
        S.dma("sp", xb[blk % 2][0:n, :], res[rr0:rr0 + n, :], T_xb[blk % 2], reads=[B_res], writes=[T_xb[blk % 2]])

    def stage_a(blk):
        r0, n = geom(blk)
        x_, Tx = xb[blk % 2], T_xb[blk % 2]
        tt, T_tt = tts[blk % ntt], T_tts[blk % ntt]
        st, T_st = sts[blk % 2], T_sts[blk % 2]
        mv, T_mv = mvs[blk % 2], T_mvs[blk % 2]
        if blk == 0:
            load_x(0)
        if blk + 1 < nblk:
            load_x(blk + 1)
        for q in range(4):
            def mm(pe, q=q):
                ins = None
                for c in range(NDC):
                    ins = pe.matmul(yps[q][0:n, :], aT[:, c, r0:r0 + n], Wo(c, q),
                                    start=(c == 0), stop=(c == NDC - 1))
                return ins
            S.op("pe", mm, reads=[T_aT] + list(T_Wo), writes=[T_y[q]])
            S.op("dve", lambda v, q=q: v.scalar_tensor_tensor(
                tt[0:n, 512 * q:512 * (q + 1)], x_[0:n, 512 * q:512 * (q + 1)], ALPHA, yps[q][0:n, :],
                ALU.mult, ALU.add), reads=[Tx, T_y[q]], pwrites=[T_tt])
            S.op("dve", lambda v, q=q: v.bn_stats(st[0:n, q, :], tt[0:n, 512 * q:512 * (q + 1)]),
                 reads=[T_tt], pwrites=[T_st])
        S.op("dve", lambda v: v.bn_aggr(mv[0:n, 0:2], st[0:n, :, :].rearrange("p a b -> p (a b)")),
             reads=[T_st], pwrites=[T_mv])
        S.op("act", lambda a: a.activation(mv[0:n, 2:3], mv[0:n, 1:2], AF.Sqrt, bias=LN_EPS, scale=1.0),
             reads=[T_mv], writes=[T_mv])
        S.op("dve", lambda v: v.reciprocal(mv[0:n, 2:3], mv[0:n, 2:3]), reads=[T_mv], writes=[T_mv])
        S.op("dve", lambda v: v.scalar_tensor_tensor(mv[0:n, 3:4], mv[0:n, 0:1], -1.0, mv[0:n, 2:3], ALU.mult, ALU.mult),
             reads=[T_mv], writes=[T_mv])
        h_, Th = hb[blk % 2], T_hb[blk % 2]
        S.op("act", lambda a: a.activation(h_[0:n, :], tt[0:n, :], AF.Identity, bias=mv[0:n, 3:4], scale=mv[0:n, 2:3]),
             reads=[T_mv, T_tt], writes=[Th])

    def stage_b(blk):
        r0, n = geom(blk)
        h_, Th = hb[blk % 2], T_hb[blk % 2]
        S.op("dve", lambda v: v.tensor_tensor(h_[0:n, :], h_[0:n, :], Gt[0:n, :], ALU.mult),
             reads=[Th, T_Gt], writes=[Th])
        S.op("pool", lambda g: g.tensor_tensor(h_[0:n, :], h_[0:n, :], Bt[0:n, :], ALU.add),
             reads=[Th, T_Bt], writes=[Th])
        S.dma("pool", dst[r0:r0 + n, :], h_[0:n, :], Th, reads=[Th], pwrites=[B_dst])

    def stage_c(blk):
        if hT is None:
            return
        r0, n = geom(blk)
        h_, Th = hb[blk % 2], T_hb[blk % 2]
        t_, Tt = hT[blk % 2], T_hT[blk % 2]
        for g4 in range(4):
            pt, Tp = tps[g4 % len(tps)], T_tp[g4 % len(tps)]

            def tr(pe, pt=pt, g4=g4):
                ins = None
                for q in range(4):
                    dc = 4 * g4 + q
                    ins = pe.transpose(pt[:, q * 128:q * 128 + n], h_[0:n, dc * 128:(dc + 1) * 128], ident[0:n, 0:n])
                return ins
            S.op("pe", tr, reads=[Th, T_ident], writes=[Tp])
            evac(t_[:, 4 * g4:4 * g4 + 4, 0:n], pt[:, :].rearrange("p (a b) -> p a b", a=4)[:, :, 0:n],
                 reads=[Tp], pwrites=[Tt], eng="act")
        S.dma("sp", H1T_s[:, :, r0:r0 + n], t_[:, :, 0:n], Tt, reads=[Tt], pwrites=[B_H1T])

    for i in range(nblk + 2):
        if 0 <= i - 2 < nblk:
            stage_c(i - 2)
        if 0 <= i - 1 < nblk:
            stage_b(i - 1)
        if i < nblk:
            stage_a(i)


def make_masks(s):
    p = np.arange(128)[:, None]
    i = np.arange(SB)[None, :]
    rowm = np.zeros((10, SB), np.float32)
    tri = np.zeros((128, 9, 128), np.float32)
    ht = np.zeros((128, 10, 16), np.float32)
    for r in range(9):
        M = np.where(128 * r - 112 + p <= 512 * s + i, 0.0, NEG).astype(np.float32)
        row = np.where((M == NEG).all(axis=0), NEG, 0.0).astype(np.float32)
        res = M - row[None, :]
        res[:, row == NEG] = 0.0
        rowm[r + 1] = row
        chk = res.copy()
        ht[:, r, :] = res[:, 0:16]
        chk[:, 0:16] = 0
        if r >= 1:
            tc = 16 + 128 * ((r - 1) % 4)
            tri[:, r, :] = res[:, tc:tc + 128]
            chk[:, tc:tc + 128] = 0
        assert not chk.any(), (s, r)
    pm = np.arange(16)[:, None]
    im = np.arange(16)[None, :]
    ht[0:16, 9, :] = np.where(pm <= 512 * s + im, 0.0, NEG)
    return rowm.reshape(1, -1), tri.reshape(128, -1), ht.reshape(128, -1)


def make_in_maps(inputs):
    x = np.asarray(inputs["x"], np.float32)
    meta = np.asarray(inputs["meta_tokens"], np.float32)
    shared = {
        "fox_w_in": np.ascontiguousarray(inputs["fox_w_in"], np.float32),
        "fox_b_f": np.ascontiguousarray(np.asarray(inputs["fox_b_f"], np.float32).reshape(NH, 1)),
        "fox_w_out": np.ascontiguousarray(inputs["fox_w_out"], np.float32),
        "ln0_g": np.ascontiguousarray(inputs["ln0_g"], np.float32),
        "ln0_b": np.ascontiguousarray(inputs["ln0_b"], np.float32),
        "pool_w_in": np.ascontiguousarray(inputs["pool_w_in"], np.float32),
        "pool_w_grp": np.ascontiguousarray(inputs["pool_w_grp"], np.float32),
        "pool_scale": np.ascontiguousarray(np.asarray(inputs["pool_scale"], np.float32).reshape(NDC, 128).T),
        "pool_w_out": np.ascontiguousarray(inputs["pool_w_out"], np.float32),
        "ln1_g": np.ascontiguousarray(inputs["ln1_g"], np.float32),
        "ln1_b": np.ascontiguousarray(inputs["ln1_b"], np.float32),
        "ident": np.eye(128, dtype=np.float32),
    }
    maps = []
    for c in range(8):
        b, s = c // 2, c % 2
        xall = np.concatenate([meta, x[b]], axis=0)
        idx = np.concatenate([np.arange(512 * (2 * j + s), 512 * (2 * j + s) + SB) for j in range(4)])
        spar = np.full((16, 1), float(s), np.float32)
        rowm, tri, ht = make_masks(s)
        m = dict(shared)
        xown = np.ascontiguousarray(xall[idx])

        def fmajor(a):
            return np.ascontiguousarray(a.T.reshape(NDC, 128, a.shape[0]).transpose(1, 0, 2))
        m.update({"xallT": fmajor(xall), "xownT": fmajor(xown), "xown": xown,
                  "spar": spar, "rowmask": rowm, "tri": tri, "ht": ht})
        maps.append(m)
    return maps


_NC_CACHE = {}


def kernel(**inputs):
    if "nc" not in _NC_CACHE:
        _NC_CACHE["nc"] = build()
    nc = _NC_CACHE["nc"]
    maps = make_in_maps(inputs)
    res = run_bass_kernel_spmd(nc, maps, core_ids=list(range(8)))
    out = np.zeros((4, 4096, D), np.float32)
    for c in range(8):
        b, s = c // 2, c % 2
        o = np.asarray(res.results[c]["out"], np.float32)
        for j in range(4):
            m = 2 * j + s
            out[b, 512 * m:512 * (m + 1), :] = o[512 * j:512 * (j + 1), :]
    return out
```
